# Optimizing a Trainium2 kernel written in Bass

```python
import math
import jax, jax.numpy as jnp
from jax import lax
import numpy as np

D_MODEL = 2048
BATCH = 1
SEQ = 8192
DEPTH = 2
DEC_BATCH = 2
DEC_SEQ = 16384
PAST_LEN = 128

HEAD_DIM = 128
HEADS_PER_GROUP = 4
DILATED_GROUPS = ((128, 1), (512, 4), (2048, 16))
N_GROUPS = len(DILATED_GROUPS)
N_ATTN_HEADS = N_GROUPS * HEADS_PER_GROUP
D_ATTN = N_ATTN_HEADS * HEAD_DIM
D_ATTN_OUT = HEADS_PER_GROUP * HEAD_DIM
Q_BLOCK = 64
NEG_INF = -1e30
D_RNN = D_MODEL
N_RNN_BLOCKS = 16
RNN_BLOCK = D_RNN // N_RNN_BLOCKS
CONV_WIDTH = 4
CONV_LEFT = 2
LRU_C = 8.0
REL_BUCKETS = 32
REL_MAX_DIST = 1024
ALPHA = (2.0 * DEPTH) ** 0.25
BETA = (8.0 * DEPTH) ** -0.25
LN_EPS = 1e-5
SPLITS = (D_ATTN, 2 * D_ATTN, 3 * D_ATTN, 3 * D_ATTN + D_ATTN_OUT,
          3 * D_ATTN + D_ATTN_OUT + D_RNN, 3 * D_ATTN + D_ATTN_OUT + 2 * D_RNN)
N_IN = 3 * D_ATTN + D_ATTN_OUT + 2 * D_RNN + 2 * D_MODEL

kernel_name = "hybrid_dilated_attn_rglru_encoder"


def t5_bucket(rel):
    nb = REL_BUCKETS // 2
    max_exact = nb // 2
    ret = (rel > 0).astype(np.int32) * nb
    n = np.abs(rel)
    large = max_exact + (np.log(np.maximum(n, max_exact) / max_exact)
                         / np.log(REL_MAX_DIST / max_exact) * (nb - max_exact)).astype(np.int32)
    large = np.minimum(large, nb - 1)
    return (ret + np.where(n < max_exact, n, large)).astype(np.int32)


def layer_norm(x, g, b):
    xf = x.astype(jnp.float32)
    mu = jnp.mean(xf, -1, keepdims=True)
    var = jnp.mean(jnp.square(xf - mu), -1, keepdims=True)
    return ((xf - mu) * lax.rsqrt(var + LN_EPS) * g + b).astype(x.dtype)


def banded_attention(q, k, v, bias_tab, half):
    N, L, H, E = q.shape
    nb = -(-L // Q_BLOCK)
    Lp = nb * Q_BLOCK
    W = Q_BLOCK + 2 * half
    qb = jnp.pad(q, ((0, 0), (0, Lp - L), (0, 0), (0, 0))).reshape(N, nb, Q_BLOCK, H, E)
    pad_kv = ((0, 0), (half, Lp - L + half), (0, 0), (0, 0))
    idx = np.arange(nb)[:, None] * Q_BLOCK + np.arange(W)[None, :]
    kb = jnp.pad(k, pad_kv)[:, idx]
    vb = jnp.pad(v, pad_kv)[:, idx].astype(jnp.float32)
    off = np.arange(W)[None, :] - half - np.arange(Q_BLOCK)[:, None]
    kpos = idx - half
    valid = (np.abs(off) <= half)[None] & ((kpos >= 0) & (kpos < L))[:, None, :]
    bias = jnp.transpose(bias_tab[np.clip(off, -half, half) + half], (2, 0, 1)).astype(jnp.float32)
    s = jnp.einsum('nbqhe,nbkhe->nbhqk', qb, kb, preferred_element_type=jnp.float32) * (E ** -0.5)
    s = jnp.where(valid[None, :, None], s + bias[None, None], NEG_INF)
    m = jnp.max(s, -1, keepdims=True)
    p = jnp.exp(s - m)
    den = jnp.sum(p, -1, keepdims=True)
    o = jnp.einsum('nbhqk,nbkhe->nbqhe', p, vb) / jnp.transpose(den, (0, 1, 3, 2, 4))
    lse = jnp.transpose((m + jnp.log(den))[..., 0], (0, 1, 3, 2))
    return o.reshape(N, Lp, H, E)[:, :L], lse.reshape(N, Lp, H)[:, :L]


def dilated_attention(q, k, v, rel_bias):
    B, S, _, E = q.shape
    outs, lses = [], []
    for g, (window, dil) in enumerate(DILATED_GROUPS):
        half = window // (2 * dil)
        L = S // dil
        hs = slice(g * HEADS_PER_GROUP, (g + 1) * HEADS_PER_GROUP)
        tab = rel_bias[t5_bucket(np.arange(-half, half + 1) * dil)][:, hs]

        def split(t):
            return t[:, :, hs].reshape(B, L, dil, HEADS_PER_GROUP, E).transpose(0, 2, 1, 3, 4) \
                .reshape(B * dil, L, HEADS_PER_GROUP, E)

        o, lse = banded_attention(split(q), split(k), split(v), tab, half)
        outs.append(o.reshape(B, dil, L, HEADS_PER_GROUP, E).transpose(0, 2, 1, 3, 4)
                    .reshape(B, S, HEADS_PER_GROUP, E))
        lses.append(lse.reshape(B, dil, L, HEADS_PER_GROUP).transpose(0, 2, 1, 3)
                    .reshape(B, S, HEADS_PER_GROUP))
    wts = jax.nn.softmax(jnp.stack(lses), axis=0)
    return jnp.sum(wts[..., None] * jnp.stack(outs), axis=0)


def _lin_combine(c1, c2):
    a1, b1 = c1
    a2, b2 = c2
    return a1 * a2, a2 * b1 + b2


def rglru(xc, w_gate, b_gate, lam, reverse):
    B, S, _ = xc.shape
    xb = xc.reshape(B, S, N_RNN_BLOCKS, RNN_BLOCK)
    gates = jax.nn.sigmoid(jnp.einsum('bsnc,gncd->gbsnd', xb, w_gate.astype(jnp.float32))
                           + b_gate.astype(jnp.float32)[:, None, None])
    r = gates[0].reshape(B, S, D_RNN)
    i = gates[1].reshape(B, S, D_RNN)
    log_a = -LRU_C * r * jax.nn.softplus(-lam.astype(jnp.float32))
    a = jnp.exp(log_a)
    b = jnp.sqrt(-jnp.expm1(2.0 * log_a)) * (i * xc)
    _, h = lax.associative_scan(_lin_combine, (a, b), axis=1, reverse=reverse)
    return h


def encoder_layer(x, w_in, b_in, conv_w, conv_b, lru_w, lru_b, lru_lam,
                  w_attn_o, w_rnn_o, w_out, ln_g, ln_b, rel_bias):
    B, S, _ = x.shape
    z = x @ w_in + b_in
    q, k, v, ga, xr, gr, gm = jnp.split(z, SPLITS, axis=-1)
    hshape = (B, S, N_ATTN_HEADS, HEAD_DIM)
    oa = dilated_attention(q.reshape(hshape), k.reshape(hshape), v.reshape(hshape), rel_bias)
    ya = (oa.reshape(B, S, D_ATTN_OUT).astype(x.dtype) * jax.nn.silu(ga)) @ w_attn_o
    xp = jnp.pad(xr, ((0, 0), (CONV_LEFT, CONV_WIDTH - 1 - CONV_LEFT), (0, 0)))
    xc = conv_b + sum(xp[:, j:j + S] * conv_w[j] for j in range(CONV_WIDTH))
    xc = xc.astype(jnp.float32)
    h = rglru(xc, lru_w[0], lru_b[0], lru_lam[0], False) + rglru(xc, lru_w[1], lru_b[1], lru_lam[1], True)
    yr = (h.astype(x.dtype) * jax.nn.silu(gr)) @ w_rnn_o
    g = jax.nn.sigmoid(gm).reshape(B, S, 2, D_MODEL)
    out = (g[:, :, 0] * ya + g[:, :, 1] * yr) @ w_out
    return layer_norm(ALPHA * x + out, ln_g, ln_b)


def setup_inputs(seed: int = 0) -> dict:
    key = jax.random.key(seed)
    ks = jax.random.split(key, 16)
    f32 = jnp.float32
    x_prompt = jax.random.normal(ks[0], (BATCH, SEQ, D_MODEL), f32)
    x_sample = jax.random.normal(ks[1], (DEC_BATCH, DEC_SEQ, D_MODEL), f32)
    w_in = jax.random.normal(ks[2], (DEPTH, D_MODEL, N_IN), f32) * D_MODEL ** -0.5
    w_in = w_in.at[:, :, 2 * D_ATTN:3 * D_ATTN].multiply(BETA)
    b_in = 0.01 * jax.random.normal(ks[3], (DEPTH, N_IN), f32)
    conv_w = jax.random.normal(ks[4], (DEPTH, CONV_WIDTH, D_RNN), f32) * CONV_WIDTH ** -0.5
    conv_b = 0.01 * jax.random.normal(ks[5], (DEPTH, D_RNN), f32)
    lru_w = jax.random.normal(ks[6], (DEPTH, 2, 2, N_RNN_BLOCKS, RNN_BLOCK, RNN_BLOCK), f32) * RNN_BLOCK ** -0.5
    lru_b = 0.01 * jax.random.normal(ks[7], (DEPTH, 2, 2, N_RNN_BLOCKS, RNN_BLOCK), f32)
    a_c = jax.random.uniform(ks[8], (DEPTH, 2, D_RNN), f32, minval=0.9, maxval=0.999)
    s = a_c ** (1.0 / LRU_C)
    lru_lam = jnp.log(s) - jnp.log1p(-s)
    w_attn_o = jax.random.normal(ks[9], (DEPTH, D_ATTN_OUT, D_MODEL), f32) * (D_ATTN_OUT ** -0.5) * BETA
    w_rnn_o = jax.random.normal(ks[10], (DEPTH, D_RNN, D_MODEL), f32) * (D_RNN ** -0.5) * BETA
    w_out = jax.random.normal(ks[11], (DEPTH, D_MODEL, D_MODEL), f32) * (D_MODEL ** -0.5) * BETA
    ln_g = 1.0 + 0.02 * jax.random.normal(ks[12], (DEPTH, D_MODEL), f32)
    ln_b = 0.02 * jax.random.normal(ks[13], (DEPTH, D_MODEL), f32)
    rel_bias = 0.1 * jax.random.normal(ks[14], (REL_BUCKETS, N_ATTN_HEADS), f32)
    return {"x_prompt": x_prompt, "x_sample": x_sample, "w_in": w_in, "b_in": b_in,
            "conv_w": conv_w, "conv_b": conv_b, "lru_w": lru_w, "lru_b": lru_b, "lru_lam": lru_lam,
            "w_attn_o": w_attn_o, "w_rnn_o": w_rnn_o, "w_out": w_out,
            "ln_g": ln_g, "ln_b": ln_b, "rel_bias": rel_bias}


def trunk(x, w_in, b_in, conv_w, conv_b, lru_w, lru_b, lru_lam, w_attn_o, w_rnn_o, w_out, ln_g, ln_b, rel_bias):
    for l in range(DEPTH):
        x = encoder_layer(x, w_in[l], b_in[l], conv_w[l], conv_b[l], lru_w[l], lru_b[l], lru_lam[l],
                          w_attn_o[l], w_rnn_o[l], w_out[l], ln_g[l], ln_b[l], rel_bias)
    return x


def reference(x_prompt, x_sample, w_in, b_in, conv_w, conv_b, lru_w, lru_b, lru_lam,
              w_attn_o, w_rnn_o, w_out, ln_g, ln_b, rel_bias):
    y_prompt = trunk(x_prompt, w_in, b_in, conv_w, conv_b, lru_w, lru_b, lru_lam,
                     w_attn_o, w_rnn_o, w_out, ln_g, ln_b, rel_bias)
    y_sample = trunk(x_sample, w_in, b_in, conv_w, conv_b, lru_w, lru_b, lru_lam,
                     w_attn_o, w_rnn_o, w_out, ln_g, ln_b, rel_bias)
    return (y_prompt, y_sample)
```

```python
import numpy as np
from contextlib import ExitStack
import concourse.bass as bass
import concourse.mybir as mybir
from concourse.bass_utils import run_bass_kernel_spmd

F32 = mybir.dt.float32
BF16 = mybir.dt.bfloat16
ALU = mybir.AluOpType
AF = mybir.ActivationFunctionType
AX = mybir.AxisListType

D = 2048
NIN = 13312
DEPTH = 2
NQKV = 4608
DILS = (1, 4, 16)
ALPHA = (2.0 * DEPTH) ** 0.25
LN_EPS = 1e-5
NEG = -30000.0
ENGS = ("sync", "act", "dve", "pool", "pe")


class Buf:
    __slots__ = ("name", "last_w", "readers", "sem", "cnt")

    def __init__(self, name):
        self.name = name
        self.last_w = None
        self.readers = []
        self.sem = None
        self.cnt = 0


class Op:
    __slots__ = ("eng", "fn", "deps", "signal", "ticket", "is_dma", "buf", "waits", "barrier", "dsem", "dticket")

    def __init__(self, eng, fn):
        self.eng = eng
        self.fn = fn
        self.deps = []
        self.signal = False
        self.ticket = 0
        self.is_dma = False
        self.buf = None
        self.waits = []
        self.barrier = False


class Sched:
    def __init__(self, nc):
        self.nc = nc
        self.ops = []

    def op(self, eng, fn, reads=(), writes=()):
        o = Op(eng, fn)
        deps = []
        for b in reads:
            if b.last_w is not None:
                deps.append((b.last_w, True))
        for b in writes:
            if b.last_w is not None:
                deps.append((b.last_w, True))
            lastr = {}
            for r in b.readers:
                lastr[r.eng if not r.is_dma else id(r)] = r
            for r in lastr.values():
                deps.append((r, False))
        for b in reads:
            b.readers.append(o)
        for b in writes:
            b.last_w = o
            b.readers = []
        seen = set()
        for d, strong in deps:
            if d is o or id(d) in seen:
                continue
            if d.eng == eng and not d.is_dma:
                if eng == "pe":
                    continue
                if not strong:
                    continue
            seen.add(id(d))
            o.deps.append(d)
        self.ops.append(o)
        return o

    def dma(self, out, in_, buf, load, slow=False, eng="sync"):
        def fn(e, out=out, in_=in_):
            if slow:
                return e.dma_start(out=out, in_=in_, allow_slow_non_contiguous=True)
            return e.dma_start(out=out, in_=in_)
        o = self.op(eng, fn, reads=() if load else (buf,), writes=(buf,) if load else ())
        o.is_dma = True
        o.buf = buf
        return o

    def barrier(self):
        o = Op(None, None)
        o.barrier = True
        self.ops.append(o)

    def finalize(self, es):
        nc = self.nc
        ops = self.ops
        last = {e: None for e in ENGS}
        for o in ops:
            if o.barrier:
                for e in ENGS:
                    if last[e] is not None and not last[e].is_dma:
                        last[e].signal = True
                continue
            for d in o.deps:
                d.signal = True
            last[o.eng] = o
        for e in ENGS:
            if last[e] is not None:
                last[e].signal = True
        esem = {e: es.enter_context(nc.semaphore("s_" + e)) for e in ENGS if e != "sync"}
        ecnt = {e: 0 for e in ENGS}
        for o in ops:
            if o.barrier:
                continue
            if o.is_dma:
                o.signal = True
            elif o.signal:
                ecnt[o.eng] += 1
                o.ticket = ecnt[o.eng]
        seen = {e: {} for e in ENGS}
        cur = {e: 0 for e in ENGS}
        pend = {e: None for e in ENGS}
        allsems = []
        free = []
        semcnt = {}
        owner = {}

        def acquire(buf):
            if free:
                s = free.pop()
            else:
                s = es.enter_context(nc.semaphore("d%d" % len(allsems)))
                allsems.append(s)
                semcnt[id(s)] = 0
            owner[id(s)] = buf
            buf.sem = s

        for o in ops:
            if o.barrier:
                snap = {}
                for e in ENGS:
                    if e != "sync" and cur[e] > 0:
                        snap[id(esem[e])] = (esem[e], cur[e])
                for s in allsems:
                    snap[id(s)] = (s, 16 * semcnt[id(s)])
                    if owner.get(id(s)) is not None:
                        owner[id(s)] = None
                        free.append(s)
                for e in ENGS:
                    pend[e] = dict(snap)
                continue
            w = {}
            if pend[o.eng] is not None:
                w.update(pend[o.eng])
                pend[o.eng] = None
            for d in o.deps:
                if d.is_dma:
                    s = d.dsem
                    if owner.get(id(s)) is d.buf:
                        v = 16 * semcnt[id(s)]
                    else:
                        v = d.dticket
                else:
                    s, v = esem[d.eng], d.ticket
                k = id(s)
                if k not in w or w[k][1] < v:
                    w[k] = (s, v)
            sd = seen[o.eng]
            for k, (s, v) in w.items():
                if sd.get(k, 0) < v:
                    sd[k] = v
                    o.waits.append((s, v))
            if o.is_dma:
                b = o.buf
                if b.sem is None or owner.get(id(b.sem)) is not b:
                    acquire(b)
                semcnt[id(b.sem)] += 1
                o.dsem = b.sem
                o.dticket = 16 * semcnt[id(b.sem)]
            elif o.signal:
                cur[o.eng] = o.ticket
        self.final_waits = [(s, 16 * semcnt[id(s)]) for s in allsems]
        self.final_eng = [(esem[e], cur[e]) for e in ENGS if e != "sync" and cur[e] > 0]
        self.esem = esem
        self.n_sems = len(allsems) + 4

    def emit(self):
        nc = self.nc
        per = {e: [o for o in self.ops if not o.barrier and o.eng == e] for e in ENGS}
        esem = self.esem

        def run(ename, eng):
            for o in per[ename]:
                for (s, v) in o.waits:
                    eng.wait_ge(s, v)
                ins = o.fn(eng)
                if o.signal:
                    if o.is_dma:
                        ins.then_inc(o.dsem, 16)
                    else:
                        ins.then_inc(esem[ename], 1)
            if ename == "sync":
                for (s, v) in self.final_waits:
                    eng.wait_ge(s, v)
                for (s, v) in self.final_eng:
                    eng.wait_ge(s, v)

        with nc.Block() as block:
            @block.sync
            def _(e):
                run("sync", e)

            @block.scalar
            def _(e):
                run("act", e)

            @block.vector
            def _(e):
                run("dve", e)

            @block.gpsimd
            def _(e):
                run("pool", e)

            @block.tensor
            def _(e):
                run("pe", e)


class Arena:
    def __init__(self, ap, words):
        self.ap = ap
        self.words = words
        self.off = 0
        self.base = 0

    def reset(self):
        self.off = self.base

    def f32(self, n):
        n8 = (n + 7) // 8 * 8
        assert self.off + n8 <= self.words, ("SBUF arena overflow", self.off, n8)
        v = self.ap[:, self.off:self.off + n]
        self.off += n8
        return v

    def bf16(self, n):
        w = (n + 1) // 2
        v = self.f32(w)
        return v.bitcast(BF16)[:, 0:n]


def build_program(T, debug=False, stop_after=None):
    assert T % 2048 == 0
    nc = bass.Bass("TRN2", target_bir_lowering=False)
    es = ExitStack()
    S = Sched(nc)

    def dram_in(name, shape, dt=F32):
        return nc.dram_tensor(name, list(shape), dt, kind="ExternalInput").ap()

    def dram_out(name, shape, dt=F32):
        return nc.dram_tensor(name, list(shape), dt, kind="ExternalOutput").ap()

    def dram_tmp(name, shape, dt, dbg=False):
        if dbg and debug:
            return nc.dram_tensor(name, list(shape), dt, kind="ExternalOutput").ap()
        return nc.dram_tensor(name, list(shape), dt).ap()

    x_in = dram_in("x", [T, D])
    w_in = dram_in("w_in", [DEPTH, D, NIN])
    b_in = dram_in("b_in", [DEPTH, NIN])
    conv_w = dram_in("conv_w", [DEPTH, 4, D])
    conv_b = dram_in("conv_b", [DEPTH, D])
    lru_w = dram_in("lru_w", [DEPTH, 2, 2, 16, 128, 128])
    lru_b = dram_in("lru_b", [DEPTH, 2, 2, 16, 128])
    lru_lam = dram_in("lru_lam", [DEPTH, 2, D])
    w_attn_o = dram_in("w_attn_o", [DEPTH, 512, D])
    w_rnn_o = dram_in("w_rnn_o", [DEPTH, D, D])
    w_out = dram_in("w_out", [DEPTH, D, D])
    ln_g = dram_in("ln_g", [DEPTH, D])
    ln_b = dram_in("ln_b", [DEPTH, D])
    ebias = dram_in("ebias", [3, 2, 128, 4, 128])
    kmask = dram_in("kmask", [128, 3 * (T // 128) + 21])
    valid = dram_in("valid", [1, T])
    ident = dram_in("ident", [128, 128])
    y_out = dram_out("y", [T, D])

    WQKV = [dram_tmp("WQKV%d" % l, [3, 128, 16, 1536], BF16) for l in range(DEPTH)]
    WF = [dram_tmp("WF%d" % l, [68, 128, 16, 128], BF16) for l in range(DEPTH)]
    WAO = [dram_tmp("WAO%d" % l, [128, 4, D], BF16) for l in range(DEPTH)]
    WRO = [dram_tmp("WRO%d" % l, [128, 16, D], BF16) for l in range(DEPTH)]
    WO = [dram_tmp("WO%d" % l, [128, 16, D], BF16) for l in range(DEPTH)]
    LW = [dram_tmp("LW%d" % l, [128, 4, 16, 128], BF16) for l in range(DEPTH)]
    XT = dram_tmp("XT", [D, T], BF16, dbg=True)
    Y1 = dram_tmp("Y1", [T, D], F32)
    QKV = dram_tmp("QKV", [3, T, 1536], BF16, dbg=True)
    OA = dram_tmp("OA", [512, T], BF16, dbg=True)
    XC = dram_tmp("XC", [D, T], F32, dbg=True)
    HF = dram_tmp("HF", [D, T], F32, dbg=True)
    HG = dram_tmp("HG", [D, T], BF16, dbg=True)
    SGA = dram_tmp("SGA", [512, T], BF16, dbg=True)
    SGR = dram_tmp("SGR", [D, T], BF16)
    GM = dram_tmp("GM", [2 * D, T], BF16)
    MT = dram_tmp("MT", [D, T], BF16, dbg=True)

    AW = 52224
    arena_t = es.enter_context(nc.sbuf_tensor("arena", [128, AW], F32))
    A = Arena(arena_t[:, :], AW)
    NBK = 6
    banks = [es.enter_context(nc.psum_tensor("ps%d" % i, [128, 512], F32)) for i in range(NBK)]
    pbuf = [Buf("ps%d" % i) for i in range(NBK)]
    tbanks = [es.enter_context(nc.psum_tensor("pt%d" % i, [128, 1024], BF16)) for i in range(2)]
    tbuf = [Buf("pt%d" % i) for i in range(2)]

    ident_t = A.f32(128)
    ident_b = Buf("ident")
    S.dma(ident_t, ident[:, :], ident_b, True)
    ident_bf = A.bf16(128)
    identbf_b = Buf("identbf")
    S.op("dve", lambda e: e.tensor_copy(out=ident_bf, in_=ident_t), reads=[ident_b], writes=[identbf_b])
    ones_bf = A.bf16(128)
    ones_b = Buf("ones")
    S.op("pool", lambda e: e.memset(ones_bf, 1.0), writes=[ones_b])
    NKM = 3 * (T // 128) + 21
    km_t = A.f32(NKM)
    km_b = Buf("km")
    S.dma(km_t, kmask[:, :], km_b, True)
    EB = A.f32(3 * 2 * 512)
    EB_b = Buf("EB")
    S.dma(EB.rearrange("p (g a f) -> p g a f", g=3, a=2),
          ebias.rearrange("g a p h q -> p g a (h q)"), EB_b, True)
    S.op("act", lambda e: e.activation(out=EB, in_=EB, func=AF.Exp), reads=[EB_b], writes=[EB_b])
    cst = []
    for l in range(DEPTH):
        c = dict(bcol=A.f32(68), cw=A.f32(64), cb=A.f32(16), lb=A.f32(64), lam=A.f32(32),
                 cf=A.f32(32), cf2=A.f32(32))
        b_ = Buf("cst%d" % l)
        c["buf"] = b_
        S.dma(c["bcol"], b_in[l, NQKV:NIN].rearrange("(c p) -> p c", p=128), b_, True, slow=True)
        for j in range(4):
            S.dma(c["cw"][:, j * 16:(j + 1) * 16], conv_w[l, j].rearrange("(n p) -> p n", p=128), b_, True, slow=True)
        S.dma(c["cb"], conv_b[l].rearrange("(n p) -> p n", p=128), b_, True, slow=True)
        S.dma(c["lb"].rearrange("p (a n) -> p a n", a=4), lru_b[l].rearrange("a b n p -> p (a b) n"),
              b_, True, slow=True)
        S.dma(c["lam"].rearrange("p (a n) -> p a n", a=2), lru_lam[l].rearrange("a (n p) -> p a n", p=128),
              b_, True, slow=True)
        S.op("act", lambda e, c=c: e.activation(out=c["cf"], in_=c["lam"], func=AF.Exp, scale=-1.0),
             reads=[b_], writes=[b_])
        S.op("act", lambda e, c=c: e.activation(out=c["cf"], in_=c["cf"], func=AF.Ln, bias=1.0),
             reads=[b_], writes=[b_])
        S.op("dve", lambda e, c=c: e.tensor_scalar(out=c["cf2"], in0=c["cf"], scalar1=-16.0, scalar2=None,
                                                   op0=ALU.mult), reads=[b_], writes=[b_])
        S.op("dve", lambda e, c=c: e.tensor_scalar(out=c["cf"], in0=c["cf"], scalar1=-8.0, scalar2=None,
                                                   op0=ALU.mult), reads=[b_], writes=[b_])
        cst.append(c)
    A.base = A.off

    cast_engs = ("dve", "pool", "act")

    def copy_fn(eng_name, out, in_):
        if eng_name == "act":
            return lambda e: e.activation(out=out, in_=in_, func=AF.Copy)
        return lambda e: e.tensor_copy(out=out, in_=in_)

    def precast():
        A.reset()
        NB = 4
        FM = 2048
        ld = [A.f32(FM) for _ in range(NB)]
        st = [A.bf16(FM) for _ in range(NB)]
        ldb = [Buf("cld%d" % i) for i in range(NB)]
        stb = [Buf("cst%d" % i) for i in range(NB)]
        cnt = [0]

        def piece(src, dst, shape):
            i = cnt[0] % NB
            en = cast_engs[cnt[0] % 2]
            cnt[0] += 1
            n = int(np.prod(shape))
            l_ap = ld[i][:, 0:n]
            s_ap = st[i][:, 0:n]
            if len(shape) == 2:
                l_ap = l_ap.rearrange("p (a b) -> p a b", a=shape[0])
                s_ap = s_ap.rearrange("p (a b) -> p a b", a=shape[0])
            if len(shape) == 2 and len(src.shape) == 2:
                S.dma(ld[i][:, 0:n], src, ldb[i], True)
            else:
                S.dma(l_ap, src, ldb[i], True)
            S.op(en, copy_fn(en, st[i][:, 0:n], ld[i][:, 0:n]), reads=[ldb[i]], writes=[stb[i]])
            S.dma(dst, s_ap, stb[i], False, eng="act")

        for l in range(DEPTH):
            for kc in range(16):
                rows = slice(kc * 128, (kc + 1) * 128)
                for j in range(3):
                    piece(w_in[l, rows, j * 1536:(j + 1) * 1536],
                          WQKV[l][0:3, :, kc, j * 512:(j + 1) * 512].rearrange("g p n -> p g n"), [3, 512])
                for g0 in range(0, 68, 16):
                    ng = min(16, 68 - g0)
                    c0 = NQKV + g0 * 128
                    piece(w_in[l, rows, c0:c0 + ng * 128],
                          WF[l][g0:g0 + ng, :, kc, :].rearrange("g p n -> p g n"), [ng, 128])
                piece(w_rnn_o[l, rows, :], WRO[l][:, kc, :], [2048])
                piece(w_out[l, rows, :], WO[l][:, kc, :], [2048])
            for kc in range(4):
                piece(w_attn_o[l, kc * 128:(kc + 1) * 128, :], WAO[l][:, kc, :], [2048])
            for dg in range(4):
                src_ = lru_w[l, dg // 2, dg % 2].rearrange("n c d -> c n d")
                piece(src_, LW[l][:, dg, :, :], [16, 128])
        S.barrier()

    def emit_xt_from_tile(xin, xin_b, t0, ntok_sub, xts, xts_b, ev, xbanks=(0, 1, 2, 3, 4, 5)):
        nsub = ntok_sub
        for fc in range(16):
            bk = xbanks[ev[0] % len(xbanks)]
            for s in range(nsub):
                S.op("pe", lambda e, bk=bk, s=s, fc=fc: e.transpose(
                    out=banks[bk][:, s * 128:(s + 1) * 128], in_=xin[:, s, fc * 128:(fc + 1) * 128],
                    identity=ident_t), reads=[xin_b, ident_b], writes=[pbuf[bk]])
            en = "act" if ev[0] % 2 == 0 else "dve"
            S.op(en, copy_fn(en, xts[:, fc, 0:nsub * 128], banks[bk][:, 0:nsub * 128]),
                 reads=[pbuf[bk]], writes=[xts_b])
            ev[0] += 1
        dst = XT.rearrange("(fc p) t -> p fc t", p=128)[:, :, t0:t0 + nsub * 128]
        S.dma(dst, xts[:, :, 0:nsub * 128], xts_b, False)

    def phase_xt0():
        A.reset()
        xin = [A.f32(4 * D).rearrange("p (s f) -> p s f", s=4) for _ in range(2)]
        xin_b = [Buf("xin%d" % i) for i in range(2)]
        xts = [A.bf16(16 * 512).rearrange("p (c t) -> p c t", c=16) for _ in range(2)]
        xts_b = [Buf("xts%d" % i) for i in range(2)]
        ev = [0]
        for i in range(T // 512):
            t0 = i * 512
            k = i % 2
            S.dma(xin[k], x_in[t0:t0 + 512, :].rearrange("(s p) f -> p s f", p=128), xin_b[k], True)
            emit_xt_from_tile(xin[k], xin_b[k], t0, 4, xts[k], xts_b[k], ev)
        S.barrier()

    def phase_qkv(l):
        for g in range(3):
            A.reset()
            wq = A.bf16(16 * 1536).rearrange("p (c n) -> p c n", c=16)
            wq_b = Buf("wq")
            bq = A.f32(1536)
            bq_b = Buf("bq")
            S.dma(wq, WQKV[l][g], wq_b, True)
            for j in range(3):
                c0 = j * 1536 + g * 512
                S.dma(bq[:, j * 512:(j + 1) * 512], b_in[l:l + 1, c0:c0 + 512].partition_broadcast(128), bq_b, True)
            xt = [A.bf16(16 * 512).rearrange("p (c t) -> p c t", c=16) for _ in range(2)]
            xt_b = [Buf("xt%d" % i) for i in range(2)]
            ost = [A.bf16(1536) for _ in range(3)]
            ost_b = [Buf("ost%d" % i) for i in range(3)]
            pc = 0
            oc = 0
            XTv = XT.rearrange("(c p) t -> p c t", p=128)
            NTq = T // 512
            S.dma(xt[0], XTv[:, :, 0:512], xt_b[0], True)
            for i in range(NTq):
                k = i % 2
                if i + 1 < NTq:
                    S.dma(xt[1 - k], XTv[:, :, (i + 1) * 512:(i + 2) * 512], xt_b[1 - k], True)
                for s in range(4):
                    o = oc % 3
                    oc += 1
                    for j in range(3):
                        bk = pc % NBK
                        pc += 1
                        for kc in range(16):
                            S.op("pe", lambda e, bk=bk, k=k, kc=kc, s=s, j=j: e.matmul(
                                banks[bk][:, :], lhsT=xt[k][:, kc, s * 128:(s + 1) * 128],
                                rhs=wq[:, kc, j * 512:(j + 1) * 512], start=(kc == 0), stop=(kc == 15)),
                                reads=[xt_b[k], wq_b], writes=[pbuf[bk]])
                        S.op("dve", lambda e, bk=bk, o=o, j=j: e.tensor_tensor(
                            out=ost[o][:, j * 512:(j + 1) * 512], in0=banks[bk][:, :],
                            in1=bq[:, j * 512:(j + 1) * 512], op=ALU.add),
                            reads=[pbuf[bk], bq_b], writes=[ost_b[o]])
                    t0 = i * 512 + s * 128
                    S.dma(QKV[g, t0:t0 + 128, :], ost[o], ost_b[o], False, eng="act")
            S.barrier()


    def phase_attn(l):
        A.reset()
        oaccs = [A.f32(4 * 2048).rearrange("p (h t) -> p h t", h=4) for _ in range(2)]
        daccs = [A.f32(4 * 2048).rearrange("p (h t) -> p h t", h=4) for _ in range(2)]
        acc_bs = [Buf("acc%d" % i) for i in range(2)]
        oab = A.bf16(4 * 2048).rearrange("p (h t) -> p h t", h=4)
        oab_b = Buf("oab")
        kv = [A.bf16(1024) for _ in range(3)]
        kv_b = [Buf("kv%d" % i) for i in range(3)]
        for i in range(3):
            S.op("pool", lambda e, i=i: e.memset(kv[i], 0.0), writes=[kv_b[i]])
        qs = [A.bf16(512) for _ in range(2)]
        qs_b = [Buf("qs%d" % i) for i in range(2)]
        KT = [A.bf16(512) for _ in range(3)]
        KT_b = [Buf("KT%d" % i) for i in range(3)]
        QT = [A.bf16(512) for _ in range(2)]
        QT_b = [Buf("QT%d" % i) for i in range(2)]
        Et = [A.f32(512) for _ in range(4)]
        Et_b = [Buf("Et%d" % i) for i in range(4)]
        Pt = [A.bf16(512) for _ in range(4)]
        Pt_b = [Buf("Pt%d" % i) for i in range(4)]
        scale = 128.0 ** -0.5
        kbase = []
        acc = 0
        for d in DILS:
            kbase.append(acc)
            acc += d * (T // d // 128 + 1)
        cnt = dict(kv=0, q=0, t=0, blk=0)
        OAv = OA.rearrange("(h p) t -> p h t", p=128)
        def kv_load(it_):
            g, d, r, jj, j, Lg, nb, QKVr, i0 = it_
            p0 = 128 * j - 64
            lo = max(p0, 0)
            hi = min(p0 + 128, Lg)
            ks = cnt["kv"] % 3
            cnt["kv"] += 1
            S.dma(kv[ks][lo - p0:hi - p0, :], QKVr[r, lo:hi, 512:1536], kv_b[ks], True)
            return ks

        def q_load(it_):
            g, d, r, jj, j, Lg, nb, QKVr, i0 = it_
            i = j - 1
            q_ = cnt["q"] % 2
            cnt["q"] += 1
            S.dma(qs[q_], QKVr[r, 128 * i:128 * i + 128, 0:512], qs_b[q_], True)
            return q_

        for sb in range(T // 2048):
            u0 = sb * 2048
            oacc, dacc, acc_b = oaccs[sb % 2], daccs[sb % 2], acc_bs[sb % 2]
            items = []
            for g, d in enumerate(DILS):
                nbs = 2048 // d // 128
                Lg = T // d
                nb = Lg // 128
                QKVr = QKV[g].rearrange("(p d) c -> d p c", d=d)
                for r in range(d):
                    i0 = sb * nbs
                    for jj in range(nbs + 1):
                        items.append((g, d, r, jj, i0 + jj, Lg, nb, QKVr, i0))
            ks_next = kv_load(items[0])
            q_next = None
            prev = None
            for idx, it_ in enumerate(items):
                g, d, r, jj, j, Lg, nb, QKVr, i0 = it_
                ks = ks_next
                q_ = q_next
                if idx + 1 < len(items):
                    ks_next = kv_load(items[idx + 1])
                    q_next = q_load(items[idx + 1]) if items[idx + 1][3] >= 1 else None
                if jj == 0:
                    prev = None
                tb = cnt["t"] % 2
                cnt["t"] += 1
                for h in range(4):
                    S.op("pe", lambda e, tb=tb, ks=ks, h=h: e.transpose(
                        out=tbanks[tb][:, h * 128:(h + 1) * 128], in_=kv[ks][:, h * 128:(h + 1) * 128],
                        identity=ident_bf), reads=[kv_b[ks], identbf_b], writes=[tbuf[tb]])
                S.op("act", lambda e, tb=tb, ks=ks: e.activation(out=KT[ks], in_=tbanks[tb][:, 0:512],
                                                                 func=AF.Copy),
                     reads=[tbuf[tb]], writes=[KT_b[ks]])
                cur = (ks, kbase[g] + r * (nb + 1) + j)
                if prev is not None:
                    i = j - 1
                    tb = cnt["t"] % 2
                    cnt["t"] += 1
                    for h in range(4):
                        S.op("pe", lambda e, tb=tb, q_=q_, h=h: e.transpose(
                            out=tbanks[tb][:, h * 128:(h + 1) * 128], in_=qs[q_][:, h * 128:(h + 1) * 128],
                            identity=ident_bf), reads=[qs_b[q_], identbf_b], writes=[tbuf[tb]])
                    S.op("dve", lambda e, tb=tb, q_=q_: e.tensor_copy(out=QT[q_], in_=tbanks[tb][:, 0:512]),
                         reads=[tbuf[tb]], writes=[QT_b[q_]])
                    par = cnt["blk"] % 2
                    cnt["blk"] += 1
                    for ab, (ksx, kcol) in enumerate((prev, cur)):
                        sbk = ab
                        ei = par * 2 + ab
                        for h in range(4):
                            S.op("pe", lambda e, sbk=sbk, ksx=ksx, q_=q_, h=h: e.matmul(
                                banks[sbk][:, h * 128:(h + 1) * 128], lhsT=KT[ksx][:, h * 128:(h + 1) * 128],
                                rhs=QT[q_][:, h * 128:(h + 1) * 128], start=True, stop=True),
                                reads=[KT_b[ksx], QT_b[q_]], writes=[pbuf[sbk]])
                        S.op("act", lambda e, sbk=sbk, ei=ei, kcol=kcol: e.activation(
                            out=Et[ei], in_=banks[sbk][:, :], func=AF.Exp, scale=scale,
                            bias=km_t[:, kcol:kcol + 1]), reads=[pbuf[sbk], km_b], writes=[Et_b[ei]])
                        S.op("pool", lambda e, ei=ei, g=g, ab=ab: e.tensor_tensor(
                            out=Pt[ei], in0=Et[ei], in1=EB[:, (g * 2 + ab) * 512:(g * 2 + ab + 1) * 512],
                            op=ALU.mult), reads=[Et_b[ei], EB_b], writes=[Pt_b[ei]])
                    ob = 2 + par * 2
                    db = 3 + par * 2
                    for h in range(4):
                        for ab, (ksx, kcol) in enumerate((prev, cur)):
                            ei = par * 2 + ab
                            S.op("pe", lambda e, ksx=ksx, ei=ei, h=h, ab=ab, ob=ob: e.matmul(
                                banks[ob][:, h * 128:(h + 1) * 128],
                                lhsT=kv[ksx][:, 512 + h * 128:512 + (h + 1) * 128],
                                rhs=Pt[ei][:, h * 128:(h + 1) * 128], start=(ab == 0), stop=(ab == 1)),
                                reads=[kv_b[ksx], Pt_b[ei]], writes=[pbuf[ob]])
                    for ab in range(2):
                        ei = par * 2 + ab
                        S.op("pe", lambda e, ei=ei, ab=ab, db=db: e.matmul(
                            banks[db][:, :], lhsT=ones_bf, rhs=Pt[ei], start=(ab == 0), stop=(ab == 1)),
                            reads=[ones_b, Pt_b[ei]], writes=[pbuf[db]])
                    c0 = 128 * (i - i0) * d + r
                    c1 = c0 + 127 * d + 1
                    ov = oacc[:, :, c0:c1:d]
                    dv = dacc[:, :, c0:c1:d]
                    po = banks[ob][:, :].rearrange("p (h q) -> p h q", h=4)
                    pd = banks[db][:, :].rearrange("p (h q) -> p h q", h=4)
                    if g == 0:
                        S.op("act", lambda e, ov=ov, po=po: e.activation(out=ov, in_=po, func=AF.Copy),
                             reads=[pbuf[ob]], writes=[acc_b])
                        S.op("dve", lambda e, dv=dv, pd=pd: e.tensor_copy(out=dv, in_=pd),
                             reads=[pbuf[db]], writes=[acc_b])
                    else:
                        S.op("dve", lambda e, ov=ov, po=po: e.tensor_tensor(out=ov, in0=po, in1=ov, op=ALU.add),
                             reads=[pbuf[ob], acc_b], writes=[acc_b])
                        S.op("dve", lambda e, dv=dv, pd=pd: e.tensor_tensor(out=dv, in0=pd, in1=dv, op=ALU.add),
                             reads=[pbuf[db], acc_b], writes=[acc_b])
                prev = cur
            S.op("dve", lambda e, dacc=dacc: e.tensor_scalar(out=dacc, in0=dacc, scalar1=1e-30, scalar2=None,
                                                             op0=ALU.max), reads=[acc_b], writes=[acc_b])
            S.op("dve", lambda e, dacc=dacc: e.reciprocal(out=dacc, in_=dacc), reads=[acc_b], writes=[acc_b])
            S.op("dve", lambda e, dacc=dacc, oacc=oacc: e.tensor_tensor(out=oab, in0=oacc, in1=dacc, op=ALU.mult),
                 reads=[acc_b], writes=[oab_b])
            S.dma(OAv[:, :, u0:u0 + 2048], oab, oab_b, False)
        S.barrier()

    def lru_s1(l, n, dirn, xc, xc_b, W, wk, LWt, LW_b, pcnt):
        c = cst[l]
        xcb, r_, i_, a_ = wk["xcb"], wk["r"], wk["i"], wk["a"]
        wb = wk["buf"]
        S.op("pool", lambda e: e.tensor_copy(out=xcb[:, 0:W], in_=xc[:, 0:W]),
             reads=[xc_b], writes=[wb["xcb"]])
        for gate, dst, dkey in ((0, r_, "r"), (1, i_, "i")):
            for c0 in range(0, W, 512):
                cw_ = min(512, W - c0)
                bk = pcnt[0] % NBK
                pcnt[0] += 1
                S.op("pe", lambda e, bk=bk, c0=c0, cw_=cw_, gate=gate: e.matmul(
                    banks[bk][:, 0:cw_], lhsT=LWt[:, dirn * 2 + gate, n, :], rhs=xcb[:, c0:c0 + cw_],
                    start=True, stop=True), reads=[wb["xcb"], LW_b], writes=[pbuf[bk]])
                col = (dirn * 2 + gate) * 16 + n
                S.op("act", lambda e, bk=bk, c0=c0, cw_=cw_, dst=dst, col=col: e.activation(
                    out=dst[:, c0:c0 + cw_], in_=banks[bk][:, 0:cw_], func=AF.Sigmoid,
                    bias=c["lb"][:, col:col + 1]), reads=[pbuf[bk], c["buf"]], writes=[wb[dkey]])
        cc = dirn * 16 + n
        S.op("act", lambda e: e.activation(out=a_[:, 0:W], in_=r_[:, 0:W], func=AF.Exp,
                                           scale=c["cf"][:, cc:cc + 1]), reads=[wb["r"], c["buf"]], writes=[wb["a"]])

    def lru_s2(l, n, dirn, xc, xc_b, W, wk, vt, vt_b):
        i_, a_, a2 = wk["i"], wk["a"], wk["a2"]
        wb = wk["buf"]
        S.op("act", lambda e: e.activation(out=a2[:, 0:W], in_=a_[:, 0:W], func=AF.Square),
             reads=[wb["a"]], writes=[wb["a2"]])
        S.op("act", lambda e: e.activation(out=a2[:, 0:W], in_=a2[:, 0:W], func=AF.Sqrt, scale=-1.0, bias=1.0),
             reads=[wb["a2"]], writes=[wb["a2"]])
        S.op("dve", lambda e: e.tensor_tensor(out=i_[:, 0:W], in0=i_[:, 0:W], in1=xc[:, 0:W], op=ALU.mult),
             reads=[wb["i"], xc_b], writes=[wb["i"]])
        if vt is not None:
            S.op("pool", lambda e: e.tensor_tensor(out=i_[:, 0:W], in0=i_[:, 0:W], in1=vt[:, 0:W], op=ALU.mult),
                 reads=[wb["i"], vt_b], writes=[wb["i"]])

    def lru_s3a(W, wk):
        i_, a2 = wk["i"], wk["a2"]
        wb = wk["buf"]
        S.op("pool", lambda e: e.tensor_tensor(out=a2[:, 0:W], in0=a2[:, 0:W], in1=i_[:, 0:W], op=ALU.mult),
             reads=[wb["a2"], wb["i"]], writes=[wb["a2"]])

    def lru_s3(l, n, dirn, W, wk, carry, carry_b, h, h_b, skip_a=False):
        i_, a_, a2 = wk["i"], wk["a"], wk["a2"]
        wb = wk["buf"]
        if not skip_a:
            lru_s3a(W, wk)
        if dirn == 0:
            S.op("dve", lambda e: e.tensor_tensor_scan(out=h[:, 0:W], data0=a_[:, 0:W], data1=a2[:, 0:W],
                                                       initial=carry[:, n:n + 1], op0=ALU.mult, op1=ALU.add),
                 reads=[wb["a"], wb["a2"], carry_b], writes=[h_b])
            S.op("dve", lambda e: e.tensor_copy(out=carry[:, n:n + 1], in_=h[:, W - 1:W]),
                 reads=[h_b], writes=[carry_b])
        else:
            S.op("dve", lambda e: e.tensor_tensor_scan(out=h[:, 0:W][:, ::-1], data0=a_[:, 0:W][:, ::-1],
                                                       data1=a2[:, 0:W][:, ::-1], initial=carry[:, n:n + 1],
                                                       op0=ALU.mult, op1=ALU.add),
                 reads=[wb["a"], wb["a2"], carry_b], writes=[h_b])
            S.op("dve", lambda e: e.tensor_copy(out=carry[:, n:n + 1], in_=h[:, 0:1]),
                 reads=[h_b], writes=[carry_b])

    def lru_work():
        wk = dict(xcb=A.bf16(1032), r=A.f32(1032), i=A.f32(1032), a=A.f32(1032), a2=A.f32(1032))
        wk["buf"] = {k: Buf("wk_" + k) for k in ("xcb", "r", "i", "a", "a2")}
        return wk

    def phase_fwd(l):
        A.reset()
        c = cst[l]
        TT = 1024
        NT = T // TT
        xt = [A.bf16(16 * TT).rearrange("p (c t) -> p c t", c=16)] * 2
        xt_b = [Buf("fxt")] * 2
        NWF = 4
        wf = [A.bf16(16 * 128).rearrange("p (c n) -> p c n", c=16) for _ in range(NWF)]
        wf_b = [Buf("wf%d" % i) for i in range(NWF)]
        LWt = A.bf16(4 * 16 * 128).rearrange("p (a n d) -> p a n d", a=4, n=16)
        LW_b = Buf("LWt")
        S.dma(LWt, LW[l], LW_b, True)
        stg = [A.bf16(TT) for _ in range(3)]
        stg_b = [Buf("stg%d" % i) for i in range(3)]
        NXB = 3
        XRb = [A.f32(1032) for _ in range(NXB)]
        XRb_b = [Buf("XRb%d" % i) for i in range(NXB)]
        xc = [A.f32(1032) for _ in range(NXB)]
        xc_b = [Buf("xc%d" % i) for i in range(NXB)]
        hh = [A.f32(1032) for _ in range(NXB)]
        hh_b = [Buf("hh%d" % i) for i in range(NXB)]
        vt = [A.f32(TT)] * 2
        vt_b = [Buf("vt")] * 2
        xrc = A.f32(48)
        xrc_b = Buf("xrc")
        hc = A.f32(16)
        hc_b = Buf("hc")
        S.op("pool", lambda e: e.memset(xrc, 0.0), writes=[xrc_b])
        S.op("pool", lambda e: e.memset(hc, 0.0), writes=[hc_b])
        for i in range(NXB):
            S.op("pool", lambda e, i=i: e.memset(XRb[i], 0.0), writes=[XRb_b[i]])
        wks = [lru_work() for _ in range(3)]
        XTv = XT.rearrange("(c p) t -> p c t", p=128)
        pcnt = [0]
        cgc = 0
        sc = 0
        xc_i = 0
        seq = []
        for n in range(16):
            seq += [("xr", n, 4 + n), ("silu", n, 20 + n), ("sig", 2 * n, 36 + 2 * n), ("sig", 2 * n + 1, 37 + 2 * n)]
        seq += [("ga", i, i) for i in range(4)]
        NS = len(seq)
        allw = [(it, p) for it in range(NT) for p in range(NS)]

        def wload(j):
            it, p = allw[j]
            S.dma(wf[j % NWF], WF[l][seq[p][2]], wf_b[j % NWF], True)

        wload(0)
        wload(1)
        for it in range(NT):
            t0 = it * TT
            last = it == NT - 1
            k = it % 2
            S.dma(xt[k], XTv[:, :, t0:t0 + TT], xt_b[k], True)
            S.dma(vt[k], valid[0:1, t0:t0 + TT].partition_broadcast(128), vt_b[k], True)
            sched = {}
            for p, (kind, idx, cidx) in enumerate(seq):
                j = it * NS + p
                wsl = j % NWF
                if j + 2 < len(allw):
                    wload(j + 2)
                if kind == "xr":
                    xi = xc_i % NXB
                    xc_i += 1
                else:
                    si = sc % 3
                    sc += 1
                for s in range(TT // 512):
                    bk = pcnt[0] % NBK
                    pcnt[0] += 1
                    for kc in range(16):
                        S.op("pe", lambda e, bk=bk, wsl=wsl, k=k, kc=kc, s=s: e.matmul(
                            banks[bk][:, :], lhsT=wf[wsl][:, kc, :],
                            rhs=xt[k][:, kc, s * 512:(s + 1) * 512], start=(kc == 0), stop=(kc == 15)),
                            reads=[wf_b[wsl], xt_b[k]], writes=[pbuf[bk]])
                    bias = c["bcol"][:, cidx:cidx + 1]
                    if kind == "xr":
                        S.op("act", lambda e, bk=bk, xi=xi, s=s, bias=bias: e.activation(
                            out=XRb[xi][:, 3 + s * 512:3 + (s + 1) * 512], in_=banks[bk][:, :],
                            func=AF.Identity, bias=bias), reads=[pbuf[bk], c["buf"]], writes=[XRb_b[xi]])
                    else:
                        fn_ = AF.Sigmoid if kind == "sig" else AF.Silu
                        S.op("act", lambda e, bk=bk, si=si, s=s, bias=bias, fn_=fn_: e.activation(
                            out=stg[si][:, s * 512:(s + 1) * 512], in_=banks[bk][:, :], func=fn_, bias=bias),
                            reads=[pbuf[bk], c["buf"]], writes=[stg_b[si]])
                for fnc in sched.pop(p, []):
                    fnc()
                if kind != "xr":
                    dst = {"ga": SGA, "silu": SGR, "sig": GM}[kind]
                    S.dma(dst[idx * 128:(idx + 1) * 128, t0:t0 + TT], stg[si], stg_b[si], False, eng="act")
                    continue
                n = idx
                X = XRb[xi]
                Xb = XRb_b[xi]
                S.op("pool", lambda e, X=X, n=n: e.tensor_copy(out=X[:, 0:3], in_=xrc[:, n * 3:(n + 1) * 3]),
                     reads=[xrc_b], writes=[Xb])
                S.op("pool", lambda e, X=X, k=k: e.tensor_tensor(out=X[:, 3:3 + TT], in0=X[:, 3:3 + TT],
                                                                in1=vt[k], op=ALU.mult),
                     reads=[Xb, vt_b[k]], writes=[Xb])
                S.op("pool", lambda e, X=X, n=n: e.tensor_copy(out=xrc[:, n * 3:(n + 1) * 3], in_=X[:, TT:TT + 3]),
                     reads=[Xb], writes=[xrc_b])
                wlo = 1 if it == 0 else 0
                whi = TT + (1 if last else 0)
                W = whi - wlo
                tok0 = t0 - 1 + wlo
                xcc = xc[xi]
                cwv = c["cw"]
                S.op("pool", lambda e, X=X, xcc=xcc, n=n, wlo=wlo, W=W: e.tensor_scalar(
                    out=xcc[:, 0:W], in0=X[:, wlo:wlo + W], scalar1=cwv[:, n:n + 1],
                    scalar2=c["cb"][:, n:n + 1], op0=ALU.mult, op1=ALU.add),
                    reads=[Xb, c["buf"]], writes=[xc_b[xi]])
                for jt in range(1, 4):
                    S.op("dve", lambda e, X=X, xcc=xcc, n=n, wlo=wlo, W=W, jt=jt: e.scalar_tensor_tensor(
                        out=xcc[:, 0:W], in0=X[:, wlo + jt:wlo + jt + W], scalar=cwv[:, jt * 16 + n:jt * 16 + n + 1],
                        in1=xcc[:, 0:W], op0=ALU.mult, op1=ALU.add),
                        reads=[Xb, c["buf"], xc_b[xi]], writes=[xc_b[xi]])
                S.dma(XC[n * 128:(n + 1) * 128, tok0:tok0 + W], xcc[:, 0:W], xc_b[xi], False, eng="pool")
                wk = wks[n % 3]

                def f1(n=n, xi=xi, W=W, wk=wk):
                    lru_s1(l, n, 0, xc[xi], xc_b[xi], W, wk, LWt, LW_b, pcnt)

                def f2(n=n, xi=xi, W=W, wk=wk):
                    lru_s2(l, n, 0, xc[xi], xc_b[xi], W, wk, None, None)

                def f3(n=n, xi=xi, W=W, wk=wk, tok0=tok0):
                    lru_s3(l, n, 0, W, wk, hc, hc_b, hh[xi], hh_b[xi])
                    S.dma(HF[n * 128:(n + 1) * 128, tok0:tok0 + W], hh[xi][:, 0:W], hh_b[xi], False, eng="pool")

                for dl, f in ((2, f1), (3, f2), (4, f3)):
                    sched.setdefault(min(p + dl, NS - 1), []).append(f)
            for p in sorted(sched):
                for fnc in sched[p]:
                    fnc()
        S.barrier()

    def phase_bwd(l):
        A.reset()
        TT = 1024
        NT = T // TT
        LWt = A.bf16(4 * 16 * 128).rearrange("p (a n d) -> p a n d", a=4, n=16)
        LW_b = Buf("LWt")
        S.dma(LWt, LW[l], LW_b, True)
        NBB = 6
        xc = [A.f32(TT) for _ in range(NBB)]
        xc_b = [Buf("bxc%d" % i) for i in range(NBB)]
        hf = [A.f32(TT) for _ in range(NBB)]
        hf_b = [Buf("bhf%d" % i) for i in range(NBB)]
        sg = [A.bf16(TT) for _ in range(NBB)]
        sg_b = [Buf("bsg%d" % i) for i in range(NBB)]
        hh = [A.f32(TT) for _ in range(2)]
        hh_b = [Buf("bhh%d" % i) for i in range(2)]
        og = [A.bf16(TT) for _ in range(2)]
        og_b = [Buf("bog%d" % i) for i in range(2)]
        vt = [A.f32(TT) for _ in range(2)]
        vt_b = [Buf("bvt%d" % i) for i in range(2)]
        hc = A.f32(16)
        hc_b = Buf("bhc")
        S.op("pool", lambda e: e.memset(hc, 0.0), writes=[hc_b])
        wks = [lru_work() for _ in range(3)]
        pcnt = [0]
        work = [(ii, it, n) for ii, it in enumerate(reversed(range(NT))) for n in range(16)]
        NWK = len(work)

        def loads(j):
            ii, it, n = work[j]
            k = j % NBB
            t0 = it * TT
            rows = slice(n * 128, (n + 1) * 128)
            if n == 0:
                S.dma(vt[ii % 2], valid[0:1, t0:t0 + TT].partition_broadcast(128), vt_b[ii % 2], True)
            S.dma(xc[k], XC[rows, t0:t0 + TT], xc_b[k], True)
            S.dma(hf[k], HF[rows, t0:t0 + TT], hf_b[k], True)
            S.dma(sg[k], SGR[rows, t0:t0 + TT], sg_b[k], True)

        loads(0)
        loads(1)
        for j in range(NWK + 2):
            if j + 2 < NWK:
                loads(j + 2)
            if 0 <= j - 2 < NWK:
                lru_s3a(TT, wks[(j - 2) % 3])
            if j < NWK:
                ii, it, n = work[j]
                lru_s1(l, n, 1, xc[j % NBB], xc_b[j % NBB], TT, wks[j % 3], LWt, LW_b, pcnt)
            if 0 <= j - 1 < NWK:
                ii, it, n = work[j - 1]
                k = (j - 1) % NBB
                lru_s2(l, n, 1, xc[k], xc_b[k], TT, wks[(j - 1) % 3], vt[ii % 2], vt_b[ii % 2])
            if 0 <= j - 2 < NWK:
                jj = j - 2
                ii, it, n = work[jj]
                k = jj % NBB
                q = jj % 2
                t0 = it * TT
                rows = slice(n * 128, (n + 1) * 128)
                lru_s3(l, n, 1, TT, wks[jj % 3], hc, hc_b, hh[q], hh_b[q], skip_a=True)
                S.op("dve", lambda e, k=k, q=q: e.tensor_tensor(out=hf[k], in0=hf[k], in1=hh[q], op=ALU.add),
                     reads=[hf_b[k], hh_b[q]], writes=[hf_b[k]])
                S.op("dve", lambda e, k=k, q=q: e.tensor_tensor(out=og[q], in0=hf[k], in1=sg[k], op=ALU.mult),
                     reads=[hf_b[k], sg_b[k]], writes=[og_b[q]])
                S.dma(HG[rows, t0:t0 + TT], og[q], og_b[q], False)
        S.barrier()

    def phase_merge(l):
        A.reset()
        TT = 512
        wao = A.bf16(4 * D).rearrange("p (c n) -> p c n", c=4)
        wro = A.bf16(16 * D).rearrange("p (c n) -> p c n", c=16)
        w_b = Buf("w4a")
        S.dma(wao, WAO[l], w_b, True)
        S.dma(wro, WRO[l], w_b, True)
        hg = [A.bf16(16 * TT).rearrange("p (c t) -> p c t", c=16) for _ in range(2)]
        hg_b = [Buf("hg%d" % i) for i in range(2)]
        oa = [A.bf16(4 * TT).rearrange("p (c t) -> p c t", c=4) for _ in range(2)]
        oa_b = [Buf("oa%d" % i) for i in range(2)]
        sga = [A.bf16(4 * TT).rearrange("p (c t) -> p c t", c=4) for _ in range(2)]
        sga_b = [Buf("sga%d" % i) for i in range(2)]
        gm = [A.bf16(2 * TT).rearrange("p (c t) -> p c t", c=2) for _ in range(4)]
        gm_b = [Buf("gm%d" % i) for i in range(4)]
        GMv = GM.rearrange("(a c p) t -> p a c t", a=2, p=128)
        gc = 0
        mt = [A.bf16(16 * TT).rearrange("p (c t) -> p c t", c=16) for _ in range(2)]
        mt_b = [Buf("mt%d" % i) for i in range(2)]
        t1 = [A.f32(TT) for _ in range(2)]
        t1_b = [Buf("t1%d" % i) for i in range(2)]
        t2 = [A.f32(TT) for _ in range(2)]
        t2_b = [Buf("t2%d" % i) for i in range(2)]
        pc = 0
        tc = 0
        NTm = T // TT

        def tloads(it):
            t0 = it * TT
            k = it % 2
            S.dma(hg[k], HG.rearrange("(c p) t -> p c t", p=128)[:, :, t0:t0 + TT], hg_b[k], True)
            S.dma(oa[k], OA.rearrange("(c p) t -> p c t", p=128)[:, :, t0:t0 + TT], oa_b[k], True)
            S.dma(sga[k], SGA.rearrange("(c p) t -> p c t", p=128)[:, :, t0:t0 + TT], sga_b[k], True)

        gwork = [(it, ft) for it in range(NTm) for ft in range(16)]

        def gload(j):
            it, ft = gwork[j]
            S.dma(gm[j % 4], GMv[:, :, ft, it * TT:(it + 1) * TT], gm_b[j % 4], True)

        tloads(0)
        gload(0)
        gload(1)
        for it in range(NTm):
            t0 = it * TT
            k = it % 2
            if it + 1 < NTm:
                tloads(it + 1)
            S.op("pool", lambda e, k=k: e.tensor_tensor(out=oa[k], in0=oa[k], in1=sga[k], op=ALU.mult),
                 reads=[oa_b[k], sga_b[k]], writes=[oa_b[k]])
            for ft in range(16):
                gj = it * 16 + ft
                gk = gj % 4
                if gj + 2 < len(gwork):
                    gload(gj + 2)
                ba = pc % NBK
                bb = (pc + 1) % NBK
                pc += 2
                for kc in range(4):
                    S.op("pe", lambda e, ba=ba, kc=kc, ft=ft, k=k: e.matmul(
                        banks[ba][:, :], lhsT=wao[:, kc, ft * 128:(ft + 1) * 128], rhs=oa[k][:, kc, :],
                        start=(kc == 0), stop=(kc == 3)), reads=[w_b, oa_b[k]], writes=[pbuf[ba]])
                for kc in range(16):
                    S.op("pe", lambda e, bb=bb, kc=kc, ft=ft, k=k: e.matmul(
                        banks[bb][:, :], lhsT=wro[:, kc, ft * 128:(ft + 1) * 128], rhs=hg[k][:, kc, :],
                        start=(kc == 0), stop=(kc == 15)), reads=[w_b, hg_b[k]], writes=[pbuf[bb]])
                q = tc % 2
                tc += 1
                S.op("dve", lambda e, ba=ba, q=q, gk=gk: e.tensor_tensor(
                    out=t1[q], in0=banks[ba][:, :], in1=gm[gk][:, 0, :], op=ALU.mult),
                    reads=[pbuf[ba], gm_b[gk]], writes=[t1_b[q]])
                S.op("dve", lambda e, bb=bb, q=q, gk=gk: e.tensor_tensor(
                    out=t2[q], in0=banks[bb][:, :], in1=gm[gk][:, 1, :], op=ALU.mult),
                    reads=[pbuf[bb], gm_b[gk]], writes=[t2_b[q]])
                S.op("pool", lambda e, q=q, k=k, ft=ft: e.tensor_tensor(
                    out=mt[k][:, ft, :], in0=t1[q], in1=t2[q], op=ALU.add),
                    reads=[t1_b[q], t2_b[q]], writes=[mt_b[k]])
            S.dma(MT.rearrange("(c p) t -> p c t", p=128)[:, :, t0:t0 + TT], mt[k], mt_b[k], False, eng="act")
        S.barrier()

    def phase_out(l, Xtok, Ydst, make_xt):
        A.reset()
        wo = A.bf16(16 * D).rearrange("p (c n) -> p c n", c=16)
        wo_b = Buf("wo")
        S.dma(wo, WO[l], wo_b, True)
        gt = A.f32(D)
        bt = A.f32(D)
        gb_b = Buf("gb")
        S.dma(gt, ln_g[l:l + 1, :].partition_broadcast(128), gb_b, True)
        S.dma(bt, ln_b[l:l + 1, :].partition_broadcast(128), gb_b, True)
        mt = [A.bf16(16 * 512).rearrange("p (c t) -> p c t", c=16) for _ in range(2)]
        mt_b = [Buf("omt%d" % i) for i in range(2)]
        xs = [A.f32(D) for _ in range(2)]
        xs_b = [Buf("oxs%d" % i) for i in range(2)]
        yt = [A.f32(2 * D).rearrange("p (s f) -> p s f", s=2) for _ in range(2)]
        yt_b = [Buf("oyt%d" % i) for i in range(2)]
        xts = [A.bf16(16 * 512).rearrange("p (c t) -> p c t", c=16) for _ in range(1)]
        xts_b = [Buf("oxts%d" % i) for i in range(1)]
        st = [A.f32(4 * 6) for _ in range(2)]
        mv = [A.f32(8) for _ in range(2)]
        st_b = [Buf("ost%d" % i) for i in range(2)]
        xc_ = 0
        ev = [0]
        NTo = T // 512
        S.dma(mt[0], MT.rearrange("(c p) t -> p c t", p=128)[:, :, 0:512], mt_b[0], True)
        S.dma(xs[0], Xtok[0:128, :], xs_b[0], True)
        for it in range(NTo):
            t0 = it * 512
            k = it % 2
            if it + 1 < NTo:
                S.dma(mt[1 - k], MT.rearrange("(c p) t -> p c t", p=128)[:, :, t0 + 512:t0 + 1024], mt_b[1 - k], True)
            for s in range(4):
                xk = xc_ % 2
                xc_ += 1
                tk = t0 + s * 128
                if tk + 128 < T:
                    S.dma(xs[1 - xk], Xtok[tk + 128:tk + 256, :], xs_b[1 - xk], True)
                yk = (it * 2 + s // 2) % 2
                yv = yt[yk][:, s % 2, :]
                for nch in range(4):
                    for kc in range(16):
                        S.op("pe", lambda e, nch=nch, kc=kc, k=k, s=s: e.matmul(
                            banks[nch][:, :], lhsT=mt[k][:, kc, s * 128:(s + 1) * 128],
                            rhs=wo[:, kc, nch * 512:(nch + 1) * 512], start=(kc == 0), stop=(kc == 15)),
                            reads=[mt_b[k], wo_b], writes=[pbuf[nch]])
                    S.op("dve", lambda e, nch=nch, xk=xk, yv=yv: e.scalar_tensor_tensor(
                        out=yv[:, nch * 512:(nch + 1) * 512], in0=xs[xk][:, nch * 512:(nch + 1) * 512], scalar=ALPHA,
                        in1=banks[nch][:, :], op0=ALU.mult, op1=ALU.add),
                        reads=[xs_b[xk], pbuf[nch]], writes=[yt_b[yk]])
                q = xk
                for nch in range(4):
                    S.op("dve", lambda e, nch=nch, q=q, yv=yv: e.bn_stats(
                        out=st[q][:, nch * 6:(nch + 1) * 6], in_=yv[:, nch * 512:(nch + 1) * 512]),
                        reads=[yt_b[yk]], writes=[st_b[q]])
                S.op("dve", lambda e, q=q: e.bn_aggr(out=mv[q][:, 0:2], in_=st[q].rearrange("p (c s) -> p c s", c=4)),
                     reads=[st_b[q]], writes=[st_b[q]])
                S.op("act", lambda e, q=q: e.activation(out=mv[q][:, 2:3], in_=mv[q][:, 1:2], func=AF.Sqrt,
                                                        bias=LN_EPS), reads=[st_b[q]], writes=[st_b[q]])
                S.op("dve", lambda e, q=q: e.reciprocal(out=mv[q][:, 2:3], in_=mv[q][:, 2:3]),
                     reads=[st_b[q]], writes=[st_b[q]])
                S.op("dve", lambda e, q=q: e.scalar_tensor_tensor(out=mv[q][:, 3:4], in0=mv[q][:, 0:1], scalar=-1.0,
                                                                  in1=mv[q][:, 2:3], op0=ALU.mult, op1=ALU.mult),
                     reads=[st_b[q]], writes=[st_b[q]])
                S.op("act", lambda e, q=q, yv=yv: e.activation(out=yv, in_=yv, func=AF.Identity,
                                                               scale=mv[q][:, 2:3], bias=mv[q][:, 3:4]),
                     reads=[yt_b[yk], st_b[q]], writes=[yt_b[yk]])
                S.op("pool", lambda e, yv=yv: e.tensor_tensor(out=yv, in0=yv, in1=gt, op=ALU.mult),
                     reads=[yt_b[yk], gb_b], writes=[yt_b[yk]])
                S.op("pool", lambda e, yv=yv: e.tensor_tensor(out=yv, in0=yv, in1=bt, op=ALU.add),
                     reads=[yt_b[yk], gb_b], writes=[yt_b[yk]])
                if s % 2 == 1:
                    th = t0 + (s // 2) * 256
                    S.dma(Ydst[th:th + 256, :].rearrange("(s p) f -> p s f", p=128), yt[yk], yt_b[yk], False, eng="act")
                    if make_xt:
                        emit_xt_from_tile(yt[yk], yt_b[yk], th, 2, xts[0], xts_b[0], ev, xbanks=(4, 5))
        S.barrier()

    STOP = stop_after
    def run_all():
        precast()
        phase_xt0()
        for l in range(DEPTH):
            Xtok = x_in if l == 0 else Y1
            Ydst = Y1 if l < DEPTH - 1 else y_out
            for nm, fn in (("qkv", lambda: phase_qkv(l)), ("attn", lambda: phase_attn(l)),
                           ("fwd", lambda: phase_fwd(l)), ("bwd", lambda: phase_bwd(l)),
                           ("merge", lambda: phase_merge(l)),
                           ("out", lambda: phase_out(l, Xtok, Ydst, l < DEPTH - 1))):
                fn()
                if STOP == (l, nm):
                    return
    run_all()

    S.finalize(es)
    S.emit()
    return nc, es


def t5_bucket(rel):
    nb = 16
    max_exact = 8
    ret = (rel > 0).astype(np.int32) * nb
    n = np.abs(rel)
    large = max_exact + (np.log(np.maximum(n, max_exact) / max_exact)
                         / np.log(1024 / max_exact) * (nb - max_exact)).astype(np.int32)
    large = np.minimum(large, nb - 1)
    return (ret + np.where(n < max_exact, n, large)).astype(np.int32)


def host_constants(rel_bias, T, seq_len):
    eb = np.full((3, 2, 128, 4, 128), NEG, np.float32)
    kl = np.arange(128)[:, None]
    ql = np.arange(128)[None, :]
    for g, d in enumerate(DILS):
        tab = rel_bias[t5_bucket(np.arange(-64, 65) * d)][:, g * 4:(g + 1) * 4]
        offA = kl - ql - 64
        offB = kl - ql + 64
        for ab, off in enumerate((offA, offB)):
            ok = np.abs(off) <= 64
            idx = np.clip(off, -64, 64) + 64
            for h in range(4):
                eb[g, ab, :, h, :] = np.where(ok, tab[idx, h], NEG)
    ncol = 3 * (T // 128) + 21
    km = np.zeros((128, ncol), np.float32)
    col = 0
    for g, d in enumerate(DILS):
        L = T // d
        nb = L // 128
        for r in range(d):
            for j in range(nb + 1):
                p = 128 * j - 64 + np.arange(128)
                tok = p * d + r
                ok = (p >= 0) & (p < L) & (tok < seq_len)
                km[:, col] = np.where(ok, 0.0, NEG)
                col += 1
    assert col == ncol
    valid = (np.arange(T) < seq_len).astype(np.float32)[None, :]
    return {"ebias": eb, "kmask": km, "valid": valid, "ident": np.eye(128, dtype=np.float32)}


WKEYS = ("w_in", "b_in", "conv_w", "conv_b", "lru_w", "lru_b", "lru_lam", "w_attn_o", "w_rnn_o",
         "w_out", "ln_g", "ln_b")


def core_inputs(x_seq, T, weights, rel_bias):
    seq_len = x_seq.shape[0]
    xp = np.zeros((T, D), np.float32)
    xp[:seq_len] = x_seq
    m = {"x": xp}
    for k in WKEYS:
        m[k] = np.ascontiguousarray(weights[k], dtype=np.float32)
    m.update(host_constants(np.asarray(rel_bias, np.float32), T, seq_len))
    return m


_CACHE = {}


def kernel(**inputs):
    T = 16384
    if T not in _CACHE:
        _CACHE[T] = build_program(T)
    nc, _es = _CACHE[T]
    weights = {k: np.asarray(inputs[k]) for k in WKEYS}
    xp = np.asarray(inputs["x_prompt"])
    xs = np.asarray(inputs["x_sample"])
    seqs = [xp[0], xs[0], xs[1]]
    in_maps = [core_inputs(s, T, weights, inputs["rel_bias"]) for s in seqs]
    res = run_bass_kernel_spmd(nc, in_maps, core_ids=[0, 1, 2])
    y_p = np.asarray(res.results[0]["y"], dtype=np.float32)[:xp.shape[1]][None]
    y_s = np.stack([np.asarray(res.results[1]["y"], dtype=np.float32),
                    np.asarray(res.results[2]["y"], dtype=np.float32)])
    return (np.ascontiguousarray(y_p), np.ascontiguousarray(y_s))
```

```python
import numpy as np
from contextlib import ExitStack
import concourse.bass as bass
import concourse.mybir as mybir
from concourse.bass_utils import run_bass_kernel_spmd

F32 = mybir.dt.float32
BF16 = mybir.dt.bfloat16
ALU = mybir.AluOpType
AF = mybir.ActivationFunctionType
AX = mybir.AxisListType

D = 2048
NIN = 13312
DEPTH = 2
NQKV = 4608
DILS = (1, 4, 16)
ALPHA = (2.0 * DEPTH) ** 0.25
LN_EPS = 1e-5
NEG = -30000.0
ENGS = ("sync", "act", "dve", "pool", "pe")


class Buf:
    __slots__ = ("name", "last_w", "readers", "sem", "cnt")

    def __init__(self, name):
        self.name = name
        self.last_w = None
        self.readers = []
        self.sem = None
        self.cnt = 0


class Op:
    __slots__ = ("eng", "fn", "deps", "signal", "ticket", "is_dma", "buf", "waits", "barrier", "dsem", "dticket")

    def __init__(self, eng, fn):
        self.eng = eng
        self.fn = fn
        self.deps = []
        self.signal = False
        self.ticket = 0
        self.is_dma = False
        self.buf = None
        self.waits = []
        self.barrier = False


class Sched:
    def __init__(self, nc):
        self.nc = nc
        self.ops = []

    def op(self, eng, fn, reads=(), writes=()):
        o = Op(eng, fn)
        deps = []
        for b in reads:
            if b.last_w is not None:
                deps.append((b.last_w, True))
        for b in writes:
            if b.last_w is not None:
                deps.append((b.last_w, True))
            lastr = {}
            for r in b.readers:
                lastr[r.eng if not r.is_dma else id(r)] = r
            for r in lastr.values():
                deps.append((r, False))
        for b in reads:
            b.readers.append(o)
        for b in writes:
            b.last_w = o
            b.readers = []
        seen = set()
        for d, strong in deps:
            if d is o or id(d) in seen:
                continue
            if d.eng == eng and not d.is_dma:
                if eng == "pe":
                    continue
                if not strong:
                    continue
            seen.add(id(d))
            o.deps.append(d)
        self.ops.append(o)
        return o

    def dma(self, out, in_, buf, load, slow=False, eng="sync"):
        def fn(e, out=out, in_=in_):
            if slow:
                return e.dma_start(out=out, in_=in_, allow_slow_non_contiguous=True)
            return e.dma_start(out=out, in_=in_)
        o = self.op(eng, fn, reads=() if load else (buf,), writes=(buf,) if load else ())
        o.is_dma = True
        o.buf = buf
        return o

    def barrier(self):
        o = Op(None, None)
        o.barrier = True
        self.ops.append(o)

    def finalize(self, es):
        nc = self.nc
        ops = self.ops
        last = {e: None for e in ENGS}
        for o in ops:
            if o.barrier:
                for e in ENGS:
                    if last[e] is not None and not last[e].is_dma:
                        last[e].signal = True
                continue
            for d in o.deps:
                d.signal = True
            last[o.eng] = o
        for e in ENGS:
            if last[e] is not None:
                last[e].signal = True
        esem = {e: es.enter_context(nc.semaphore("s_" + e)) for e in ENGS if e != "sync"}
        ecnt = {e: 0 for e in ENGS}
        for o in ops:
            if o.barrier:
                continue
            if o.is_dma:
                o.signal = True
            elif o.signal:
                ecnt[o.eng] += 1
                o.ticket = ecnt[o.eng]
        seen = {e: {} for e in ENGS}
        cur = {e: 0 for e in ENGS}
        pend = {e: None for e in ENGS}
        allsems = []
        free = []
        semcnt = {}
        owner = {}

        def acquire(buf):
            if free:
                s = free.pop()
            else:
                s = es.enter_context(nc.semaphore("d%d" % len(allsems)))
                allsems.append(s)
                semcnt[id(s)] = 0
            owner[id(s)] = buf
            buf.sem = s

        for o in ops:
            if o.barrier:
                snap = {}
                for e in ENGS:
                    if e != "sync" and cur[e] > 0:
                        snap[id(esem[e])] = (esem[e], cur[e])
                for s in allsems:
                    snap[id(s)] = (s, 16 * semcnt[id(s)])
                    if owner.get(id(s)) is not None:
                        owner[id(s)] = None
                        free.append(s)
                for e in ENGS:
                    pend[e] = dict(snap)
                continue
            w = {}
            if pend[o.eng] is not None:
                w.update(pend[o.eng])
                pend[o.eng] = None
            for d in o.deps:
                if d.is_dma:
                    s = d.dsem
                    if owner.get(id(s)) is d.buf:
                        v = 16 * semcnt[id(s)]
                    else:
                        v = d.dticket
                else:
                    s, v = esem[d.eng], d.ticket
                k = id(s)
                if k not in w or w[k][1] < v:
                    w[k] = (s, v)
            sd = seen[o.eng]
            for k, (s, v) in w.items():
                if sd.get(k, 0) < v:
                    sd[k] = v
                    o.waits.append((s, v))
            if o.is_dma:
                b = o.buf
                if b.sem is None or owner.get(id(b.sem)) is not b:
                    acquire(b)
                semcnt[id(b.sem)] += 1
                o.dsem = b.sem
                o.dticket = 16 * semcnt[id(b.sem)]
            elif o.signal:
                cur[o.eng] = o.ticket
        self.final_waits = [(s, 16 * semcnt[id(s)]) for s in allsems]
        self.final_eng = [(esem[e], cur[e]) for e in ENGS if e != "sync" and cur[e] > 0]
        self.esem = esem
        self.n_sems = len(allsems) + 4

    def emit(self):
        nc = self.nc
        per = {e: [o for o in self.ops if not o.barrier and o.eng == e] for e in ENGS}
        esem = self.esem

        def run(ename, eng):
            for o in per[ename]:
                for (s, v) in o.waits:
                    eng.wait_ge(s, v)
                ins = o.fn(eng)
                if o.signal:
                    if o.is_dma:
                        ins.then_inc(o.dsem, 16)
                    else:
                        ins.then_inc(esem[ename], 1)
            if ename == "sync":
                for (s, v) in self.final_waits:
                    eng.wait_ge(s, v)
                for (s, v) in self.final_eng:
                    eng.wait_ge(s, v)

        with nc.Block() as block:
            @block.sync
            def _(e):
                run("sync", e)

            @block.scalar
            def _(e):
                run("act", e)

            @block.vector
            def _(e):
                run("dve", e)

            @block.gpsimd
            def _(e):
                run("pool", e)

            @block.tensor
            def _(e):
                run("pe", e)


class Arena:
    def __init__(self, ap, words):
        self.ap = ap
        self.words = words
        self.off = 0
        self.base = 0

    def reset(self):
        self.off = self.base

    def f32(self, n):
        n8 = (n + 7) // 8 * 8
        assert self.off + n8 <= self.words, ("SBUF arena overflow", self.off, n8)
        v = self.ap[:, self.off:self.off + n]
        self.off += n8
        return v

    def bf16(self, n):
        w = (n + 1) // 2
        v = self.f32(w)
        return v.bitcast(BF16)[:, 0:n]


def build_program(T, debug=False, stop_after=None):
    assert T % 2048 == 0
    nc = bass.Bass("TRN2", target_bir_lowering=False)
    es = ExitStack()
    S = Sched(nc)

    def dram_in(name, shape, dt=F32):
        return nc.dram_tensor(name, list(shape), dt, kind="ExternalInput").ap()

    def dram_out(name, shape, dt=F32):
        return nc.dram_tensor(name, list(shape), dt, kind="ExternalOutput").ap()

    def dram_tmp(name, shape, dt, dbg=False):
        if dbg and debug:
            return nc.dram_tensor(name, list(shape), dt, kind="ExternalOutput").ap()
        return nc.dram_tensor(name, list(shape), dt).ap()

    x_in = dram_in("x", [T, D])
    w_in = dram_in("w_in", [DEPTH, D, NIN])
    b_in = dram_in("b_in", [DEPTH, NIN])
    conv_w = dram_in("conv_w", [DEPTH, 4, D])
    conv_b = dram_in("conv_b", [DEPTH, D])
    lru_w = dram_in("lru_w", [DEPTH, 2, 2, 16, 128, 128])
    lru_b = dram_in("lru_b", [DEPTH, 2, 2, 16, 128])
    lru_lam = dram_in("lru_lam", [DEPTH, 2, D])
    w_attn_o = dram_in("w_attn_o", [DEPTH, 512, D])
    w_rnn_o = dram_in("w_rnn_o", [DEPTH, D, D])
    w_out = dram_in("w_out", [DEPTH, D, D])
    ln_g = dram_in("ln_g", [DEPTH, D])
    ln_b = dram_in("ln_b", [DEPTH, D])
    ebias = dram_in("ebias", [3, 2, 128, 4, 128])
    kmask = dram_in("kmask", [128, 3 * (T // 128) + 21])
    valid = dram_in("valid", [1, T])
    ident = dram_in("ident", [128, 128])
    y_out = dram_out("y", [T, D])

    WQKV = [dram_tmp("WQKV%d" % l, [3, 128, 16, 1536], BF16) for l in range(DEPTH)]
    WF = [dram_tmp("WF%d" % l, [68, 128, 16, 128], BF16) for l in range(DEPTH)]
    WAO = [dram_tmp("WAO%d" % l, [128, 4, D], BF16) for l in range(DEPTH)]
    WRO = [dram_tmp("WRO%d" % l, [128, 16, D], BF16) for l in range(DEPTH)]
    WO = [dram_tmp("WO%d" % l, [128, 16, D], BF16) for l in range(DEPTH)]
    LW = [dram_tmp("LW%d" % l, [128, 4, 16, 128], BF16) for l in range(DEPTH)]
    XT = dram_tmp("XT", [D, T], BF16, dbg=True)
    Y1 = dram_tmp("Y1", [T, D], F32)
    QKV = dram_tmp("QKV", [3, T, 1536], BF16, dbg=True)
    OA = dram_tmp("OA", [512, T], BF16, dbg=True)
    XC = dram_tmp("XC", [D, T], F32, dbg=True)
    HF = dram_tmp("HF", [D, T], F32, dbg=True)
    HG = dram_tmp("HG", [D, T], BF16, dbg=True)
    SGA = dram_tmp("SGA", [512, T], BF16, dbg=True)
    SGR = dram_tmp("SGR", [D, T], BF16)
    GM = dram_tmp("GM", [2 * D, T], BF16)
    MT = dram_tmp("MT", [D, T], BF16, dbg=True)

    AW = 52224
    arena_t = es.enter_context(nc.sbuf_tensor("arena", [128, AW], F32))
    A = Arena(arena_t[:, :], AW)
    NBK = 6
    banks = [es.enter_context(nc.psum_tensor("ps%d" % i, [128, 512], F32)) for i in range(NBK)]
    pbuf = [Buf("ps%d" % i) for i in range(NBK)]
    tbanks = [es.enter_context(nc.psum_tensor("pt%d" % i, [128, 1024], BF16)) for i in range(2)]
    tbuf = [Buf("pt%d" % i) for i in range(2)]

    ident_t = A.f32(128)
    ident_b = Buf("ident")
    S.dma(ident_t, ident[:, :], ident_b, True)
    ident_bf = A.bf16(128)
    identbf_b = Buf("identbf")
    S.op("dve", lambda e: e.tensor_copy(out=ident_bf, in_=ident_t), reads=[ident_b], writes=[identbf_b])
    ones_bf = A.bf16(128)
    ones_b = Buf("ones")
    S.op("pool", lambda e: e.memset(ones_bf, 1.0), writes=[ones_b])
    NKM = 3 * (T // 128) + 21
    km_t = A.f32(NKM)
    km_b = Buf("km")
    S.dma(km_t, kmask[:, :], km_b, True)
    EB = A.f32(3 * 2 * 512)
    EB_b = Buf("EB")
    S.dma(EB.rearrange("p (g a f) -> p g a f", g=3, a=2),
          ebias.rearrange("g a p h q -> p g a (h q)"), EB_b, True)
    S.op("act", lambda e: e.activation(out=EB, in_=EB, func=AF.Exp), reads=[EB_b], writes=[EB_b])
    cst = []
    for l in range(DEPTH):
        c = dict(bcol=A.f32(68), cw=A.f32(64), cb=A.f32(16), lb=A.f32(64), lam=A.f32(32),
                 cf=A.f32(32), cf2=A.f32(32))
        b_ = Buf("cst%d" % l)
        c["buf"] = b_
        S.dma(c["bcol"], b_in[l, NQKV:NIN].rearrange("(c p) -> p c", p=128), b_, True, slow=True)
        for j in range(4):
            S.dma(c["cw"][:, j * 16:(j + 1) * 16], conv_w[l, j].rearrange("(n p) -> p n", p=128), b_, True, slow=True)
        S.dma(c["cb"], conv_b[l].rearrange("(n p) -> p n", p=128), b_, True, slow=True)
        S.dma(c["lb"].rearrange("p (a n) -> p a n", a=4), lru_b[l].rearrange("a b n p -> p (a b) n"),
              b_, True, slow=True)
        S.dma(c["lam"].rearrange("p (a n) -> p a n", a=2), lru_lam[l].rearrange("a (n p) -> p a n", p=128),
              b_, True, slow=True)
        S.op("act", lambda e, c=c: e.activation(out=c["cf"], in_=c["lam"], func=AF.Exp, scale=-1.0),
             reads=[b_], writes=[b_])
        S.op("act", lambda e, c=c: e.activation(out=c["cf"], in_=c["cf"], func=AF.Ln, bias=1.0),
             reads=[b_], writes=[b_])
        S.op("dve", lambda e, c=c: e.tensor_scalar(out=c["cf2"], in0=c["cf"], scalar1=-16.0, scalar2=None,
                                                   op0=ALU.mult), reads=[b_], writes=[b_])
        S.op("dve", lambda e, c=c: e.tensor_scalar(out=c["cf"], in0=c["cf"], scalar1=-8.0, scalar2=None,
                                                   op0=ALU.mult), reads=[b_], writes=[b_])
        cst.append(c)
    A.base = A.off

    cast_engs = ("dve", "pool", "act")

    def copy_fn(eng_name, out, in_):
        if eng_name == "act":
            return lambda e: e.activation(out=out, in_=in_, func=AF.Copy)
        return lambda e: e.tensor_copy(out=out, in_=in_)

    def precast():
        A.reset()
        NB = 4
        FM = 2048
        ld = [A.f32(FM) for _ in range(NB)]
        st = [A.bf16(FM) for _ in range(NB)]
        ldb = [Buf("cld%d" % i) for i in range(NB)]
        stb = [Buf("cst%d" % i) for i in range(NB)]
        cnt = [0]

        def piece(src, dst, shape):
            i = cnt[0] % NB
            en = cast_engs[cnt[0] % 2]
            cnt[0] += 1
            n = int(np.prod(shape))
            l_ap = ld[i][:, 0:n]
            s_ap = st[i][:, 0:n]
            if len(shape) == 2:
                l_ap = l_ap.rearrange("p (a b) -> p a b", a=shape[0])
                s_ap = s_ap.rearrange("p (a b) -> p a b", a=shape[0])
            if len(shape) == 2 and len(src.shape) == 2:
                S.dma(ld[i][:, 0:n], src, ldb[i], True)
            else:
                S.dma(l_ap, src, ldb[i], True)
            S.op(en, copy_fn(en, st[i][:, 0:n], ld[i][:, 0:n]), reads=[ldb[i]], writes=[stb[i]])
            S.dma(dst, s_ap, stb[i], False, eng="act")

        for l in range(DEPTH):
            for kc in range(16):
                rows = slice(kc * 128, (kc + 1) * 128)
                for j in range(3):
                    piece(w_in[l, rows, j * 1536:(j + 1) * 1536],
                          WQKV[l][0:3, :, kc, j * 512:(j + 1) * 512].rearrange("g p n -> p g n"), [3, 512])
                for g0 in range(0, 68, 16):
                    ng = min(16, 68 - g0)
                    c0 = NQKV + g0 * 128
                    piece(w_in[l, rows, c0:c0 + ng * 128],
                          WF[l][g0:g0 + ng, :, kc, :].rearrange("g p n -> p g n"), [ng, 128])
                piece(w_rnn_o[l, rows, :], WRO[l][:, kc, :], [2048])
                piece(w_out[l, rows, :], WO[l][:, kc, :], [2048])
            for kc in range(4):
                piece(w_attn_o[l, kc * 128:(kc + 1) * 128, :], WAO[l][:, kc, :], [2048])
            for dg in range(4):
                src_ = lru_w[l, dg // 2, dg % 2].rearrange("n c d -> c n d")
                piece(src_, LW[l][:, dg, :, :], [16, 128])
        S.barrier()

    def emit_xt_from_tile(xin, xin_b, t0, ntok_sub, xts, xts_b, ev, xbanks=(0, 1, 2, 3, 4, 5)):
        nsub = ntok_sub
        for fc in range(16):
            bk = xbanks[ev[0] % len(xbanks)]
            for s in range(nsub):
                S.op("pe", lambda e, bk=bk, s=s, fc=fc: e.transpose(
                    out=banks[bk][:, s * 128:(s + 1) * 128], in_=xin[:, s, fc * 128:(fc + 1) * 128],
                    identity=ident_t), reads=[xin_b, ident_b], writes=[pbuf[bk]])
            en = "act" if ev[0] % 2 == 0 else "dve"
            S.op(en, copy_fn(en, xts[:, fc, 0:nsub * 128], banks[bk][:, 0:nsub * 128]),
                 reads=[pbuf[bk]], writes=[xts_b])
            ev[0] += 1
        dst = XT.rearrange("(fc p) t -> p fc t", p=128)[:, :, t0:t0 + nsub * 128]
        S.dma(dst, xts[:, :, 0:nsub * 128], xts_b, False)

    def phase_xt0():
        A.reset()
        xin = [A.f32(4 * D).rearrange("p (s f) -> p s f", s=4) for _ in range(2)]
        xin_b = [Buf("xin%d" % i) for i in range(2)]
        xts = [A.bf16(16 * 512).rearrange("p (c t) -> p c t", c=16) for _ in range(2)]
        xts_b = [Buf("xts%d" % i) for i in range(2)]
        ev = [0]
        for i in range(T // 512):
            t0 = i * 512
            k = i % 2
            S.dma(xin[k], x_in[t0:t0 + 512, :].rearrange("(s p) f -> p s f", p=128), xin_b[k], True)
            emit_xt_from_tile(xin[k], xin_b[k], t0, 4, xts[k], xts_b[k], ev)
        S.barrier()

    def phase_qkv(l):
        for g in range(3):
            A.reset()
            wq = A.bf16(16 * 1536).rearrange("p (c n) -> p c n", c=16)
            wq_b = Buf("wq")
            bq = A.f32(1536)
            bq_b = Buf("bq")
            S.dma(wq, WQKV[l][g], wq_b, True)
            for j in range(3):
                c0 = j * 1536 + g * 512
                S.dma(bq[:, j * 512:(j + 1) * 512], b_in[l:l + 1, c0:c0 + 512].partition_broadcast(128), bq_b, True)
            xt = [A.bf16(16 * 512).rearrange("p (c t) -> p c t", c=16) for _ in range(2)]
            xt_b = [Buf("xt%d" % i) for i in range(2)]
            ost = [A.bf16(1536) for _ in range(3)]
            ost_b = [Buf("ost%d" % i) for i in range(3)]
            pc = 0
            oc = 0
            XTv = XT.rearrange("(c p) t -> p c t", p=128)
            NTq = T // 512
            S.dma(xt[0], XTv[:, :, 0:512], xt_b[0], True)
            for i in range(NTq):
                k = i % 2
                if i + 1 < NTq:
                    S.dma(xt[1 - k], XTv[:, :, (i + 1) * 512:(i + 2) * 512], xt_b[1 - k], True)
                for s in range(4):
                    o = oc % 3
                    oc += 1
                    for j in range(3):
                        bk = pc % NBK
                        pc += 1
                        for kc in range(16):
                            S.op("pe", lambda e, bk=bk, k=k, kc=kc, s=s, j=j: e.matmul(
                                banks[bk][:, :], lhsT=xt[k][:, kc, s * 128:(s + 1) * 128],
                                rhs=wq[:, kc, j * 512:(j + 1) * 512], start=(kc == 0), stop=(kc == 15)),
                                reads=[xt_b[k], wq_b], writes=[pbuf[bk]])
                        S.op("dve", lambda e, bk=bk, o=o, j=j: e.tensor_tensor(
                            out=ost[o][:, j * 512:(j + 1) * 512], in0=banks[bk][:, :],
                            in1=bq[:, j * 512:(j + 1) * 512], op=ALU.add),
                            reads=[pbuf[bk], bq_b], writes=[ost_b[o]])
                    t0 = i * 512 + s * 128
                    S.dma(QKV[g, t0:t0 + 128, :], ost[o], ost_b[o], False, eng="act")
            S.barrier()


    def phase_attn(l):
        A.reset()
        oaccs = [A.f32(4 * 2048).rearrange("p (h t) -> p h t", h=4) for _ in range(2)]
        daccs = [A.f32(4 * 2048).rearrange("p (h t) -> p h t", h=4) for _ in range(2)]
        acc_bs = [Buf("acc%d" % i) for i in range(2)]
        oab = A.bf16(4 * 2048).rearrange("p (h t) -> p h t", h=4)
        oab_b = Buf("oab")
        NKV = 4
        kv = [A.bf16(1024) for _ in range(NKV)]
        kv_b = [Buf("kv%d" % i) for i in range(NKV)]
        for i in range(NKV):
            S.op("pool", lambda e, i=i: e.memset(kv[i], 0.0), writes=[kv_b[i]])
        qs = [A.bf16(512) for _ in range(2)]
        qs_b = [Buf("qs%d" % i) for i in range(2)]
        KT = [A.bf16(512) for _ in range(4)]
        KT_b = [Buf("KT%d" % i) for i in range(4)]
        QT = [A.bf16(512) for _ in range(2)]
        QT_b = [Buf("QT%d" % i) for i in range(2)]
        Et = [A.f32(512) for _ in range(4)]
        Et_b = [Buf("Et%d" % i) for i in range(4)]
        Pt = [A.bf16(512) for _ in range(4)]
        Pt_b = [Buf("Pt%d" % i) for i in range(4)]
        scale = 128.0 ** -0.5
        kbase = []
        acc = 0
        for d in DILS:
            kbase.append(acc)
            acc += d * (T // d // 128 + 1)
        cnt = dict(kv=0, q=0, t=0, blk=0)
        OAv = OA.rearrange("(h p) t -> p h t", p=128)
        def kv_load(it_):
            g, d, r, jj, j, Lg, nb, QKVr, i0 = it_
            p0 = 128 * j - 64
            lo = max(p0, 0)
            hi = min(p0 + 128, Lg)
            ks = cnt["kv"] % NKV
            cnt["kv"] += 1
            S.dma(kv[ks][lo - p0:hi - p0, :], QKVr[r, lo:hi, 512:1536], kv_b[ks], True)
            return ks

        def q_load(it_):
            g, d, r, jj, j, Lg, nb, QKVr, i0 = it_
            i = j - 1
            q_ = cnt["q"] % 2
            cnt["q"] += 1
            S.dma(qs[q_], QKVr[r, 128 * i:128 * i + 128, 0:512], qs_b[q_], True)
            return q_

        for sb in range(T // 2048):
            u0 = sb * 2048
            oacc, dacc, acc_b = oaccs[sb % 2], daccs[sb % 2], acc_bs[sb % 2]
            items = []
            for g, d in enumerate(DILS):
                nbs = 2048 // d // 128
                Lg = T // d
                nb = Lg // 128
                QKVr = QKV[g].rearrange("(p d) c -> d p c", d=d)
                for r in range(d):
                    i0 = sb * nbs
                    for jj in range(nbs + 1):
                        items.append((g, d, r, jj, i0 + jj, Lg, nb, QKVr, i0))
            ks_next = kv_load(items[0])
            q_next = None
            prev = None
            deferred = []
            for idx, it_ in enumerate(items):
                g, d, r, jj, j, Lg, nb, QKVr, i0 = it_
                ks = ks_next
                q_ = q_next
                if jj == 0:
                    while deferred:
                        deferred.pop(0)()
                if idx + 1 < len(items):
                    ks_next = kv_load(items[idx + 1])
                    q_next = q_load(items[idx + 1]) if items[idx + 1][3] >= 1 else None
                if jj == 0:
                    prev = None
                tb = cnt["t"] % 2
                cnt["t"] += 1
                for h in range(4):
                    S.op("pe", lambda e, tb=tb, ks=ks, h=h: e.transpose(
                        out=tbanks[tb][:, h * 128:(h + 1) * 128], in_=kv[ks][:, h * 128:(h + 1) * 128],
                        identity=ident_bf), reads=[kv_b[ks], identbf_b], writes=[tbuf[tb]])
                S.op("act", lambda e, tb=tb, ks=ks: e.activation(out=KT[ks], in_=tbanks[tb][:, 0:512],
                                                                 func=AF.Copy),
                     reads=[tbuf[tb]], writes=[KT_b[ks]])
                cur = (ks, kbase[g] + r * (nb + 1) + j)
                if prev is not None:
                    i = j - 1
                    tb = cnt["t"] % 2
                    cnt["t"] += 1
                    for h in range(4):
                        S.op("pe", lambda e, tb=tb, q_=q_, h=h: e.transpose(
                            out=tbanks[tb][:, h * 128:(h + 1) * 128], in_=qs[q_][:, h * 128:(h + 1) * 128],
                            identity=ident_bf), reads=[qs_b[q_], identbf_b], writes=[tbuf[tb]])
                    S.op("dve", lambda e, tb=tb, q_=q_: e.tensor_copy(out=QT[q_], in_=tbanks[tb][:, 0:512]),
                         reads=[tbuf[tb]], writes=[QT_b[q_]])
                    par = cnt["blk"] % 2
                    cnt["blk"] += 1
                    for ab, (ksx, kcol) in enumerate((prev, cur)):
                        sbk = ab
                        ei = par * 2 + ab
                        for h in range(4):
                            S.op("pe", lambda e, sbk=sbk, ksx=ksx, q_=q_, h=h: e.matmul(
                                banks[sbk][:, h * 128:(h + 1) * 128], lhsT=KT[ksx][:, h * 128:(h + 1) * 128],
                                rhs=QT[q_][:, h * 128:(h + 1) * 128], start=True, stop=True),
                                reads=[KT_b[ksx], QT_b[q_]], writes=[pbuf[sbk]])
                        S.op("act", lambda e, sbk=sbk, ei=ei, kcol=kcol: e.activation(
                            out=Et[ei], in_=banks[sbk][:, :], func=AF.Exp, scale=scale,
                            bias=km_t[:, kcol:kcol + 1]), reads=[pbuf[sbk], km_b], writes=[Et_b[ei]])
                        S.op("pool", lambda e, ei=ei, g=g, ab=ab: e.tensor_tensor(
                            out=Pt[ei], in0=Et[ei], in1=EB[:, (g * 2 + ab) * 512:(g * 2 + ab + 1) * 512],
                            op=ALU.mult), reads=[Et_b[ei], EB_b], writes=[Pt_b[ei]])
                    ob = 2 + par * 2
                    db = 3 + par * 2

                    def pv_stage(prev=prev, cur=cur, par=par, ob=ob, db=db, g=g, d=d, r=r, i=i, i0=i0,
                                 oacc=oacc, dacc=dacc, acc_b=acc_b):
                        for h in range(4):
                            for ab, (ksx, kcol) in enumerate((prev, cur)):
                                ei = par * 2 + ab
                                S.op("pe", lambda e, ksx=ksx, ei=ei, h=h, ab=ab, ob=ob: e.matmul(
                                    banks[ob][:, h * 128:(h + 1) * 128],
                                    lhsT=kv[ksx][:, 512 + h * 128:512 + (h + 1) * 128],
                                    rhs=Pt[ei][:, h * 128:(h + 1) * 128], start=(ab == 0), stop=(ab == 1)),
                                    reads=[kv_b[ksx], Pt_b[ei]], writes=[pbuf[ob]])
                        for ab in range(2):
                            ei = par * 2 + ab
                            S.op("pe", lambda e, ei=ei, ab=ab, db=db: e.matmul(
                                banks[db][:, :], lhsT=ones_bf, rhs=Pt[ei], start=(ab == 0), stop=(ab == 1)),
                                reads=[ones_b, Pt_b[ei]], writes=[pbuf[db]])
                        c0 = 128 * (i - i0) * d + r
                        c1 = c0 + 127 * d + 1
                        ov = oacc[:, :, c0:c1:d]
                        dv = dacc[:, :, c0:c1:d]
                        po = banks[ob][:, :].rearrange("p (h q) -> p h q", h=4)
                        pd = banks[db][:, :].rearrange("p (h q) -> p h q", h=4)
                        if g == 0:
                            S.op("act", lambda e, ov=ov, po=po: e.activation(out=ov, in_=po, func=AF.Copy),
                                 reads=[pbuf[ob]], writes=[acc_b])
                            S.op("dve", lambda e, dv=dv, pd=pd: e.tensor_copy(out=dv, in_=pd),
                                 reads=[pbuf[db]], writes=[acc_b])
                        else:
                            S.op("dve", lambda e, ov=ov, po=po: e.tensor_tensor(out=ov, in0=po, in1=ov, op=ALU.add),
                                 reads=[pbuf[ob], acc_b], writes=[acc_b])
                            S.op("dve", lambda e, dv=dv, pd=pd: e.tensor_tensor(out=dv, in0=pd, in1=dv, op=ALU.add),
                                 reads=[pbuf[db], acc_b], writes=[acc_b])

                    while deferred:
                        deferred.pop(0)()
                    deferred.append(pv_stage)
                    prev = cur
                    continue
                    for h in range(4):
                        for ab, (ksx, kcol) in enumerate((prev, cur)):
                            ei = par * 2 + ab
                            S.op("pe", lambda e, ksx=ksx, ei=ei, h=h, ab=ab, ob=ob: e.matmul(
                                banks[ob][:, h * 128:(h + 1) * 128],
                                lhsT=kv[ksx][:, 512 + h * 128:512 + (h + 1) * 128],
                                rhs=Pt[ei][:, h * 128:(h + 1) * 128], start=(ab == 0), stop=(ab == 1)),
                                reads=[kv_b[ksx], Pt_b[ei]], writes=[pbuf[ob]])
                    for ab in range(2):
                        ei = par * 2 + ab
                        S.op("pe", lambda e, ei=ei, ab=ab, db=db: e.matmul(
                            banks[db][:, :], lhsT=ones_bf, rhs=Pt[ei], start=(ab == 0), stop=(ab == 1)),
                            reads=[ones_b, Pt_b[ei]], writes=[pbuf[db]])
                    c0 = 128 * (i - i0) * d + r
                    c1 = c0 + 127 * d + 1
                    ov = oacc[:, :, c0:c1:d]
                    dv = dacc[:, :, c0:c1:d]
                    po = banks[ob][:, :].rearrange("p (h q) -> p h q", h=4)
                    pd = banks[db][:, :].rearrange("p (h q) -> p h q", h=4)
                    if g == 0:
                        S.op("act", lambda e, ov=ov, po=po: e.activation(out=ov, in_=po, func=AF.Copy),
                             reads=[pbuf[ob]], writes=[acc_b])
                        S.op("dve", lambda e, dv=dv, pd=pd: e.tensor_copy(out=dv, in_=pd),
                             reads=[pbuf[db]], writes=[acc_b])
                    else:
                        S.op("dve", lambda e, ov=ov, po=po: e.tensor_tensor(out=ov, in0=po, in1=ov, op=ALU.add),
                             reads=[pbuf[ob], acc_b], writes=[acc_b])
                        S.op("dve", lambda e, dv=dv, pd=pd: e.tensor_tensor(out=dv, in0=pd, in1=dv, op=ALU.add),
                             reads=[pbuf[db], acc_b], writes=[acc_b])
                prev = cur
            while deferred:
                deferred.pop(0)()
            S.op("dve", lambda e, dacc=dacc: e.tensor_scalar(out=dacc, in0=dacc, scalar1=1e-30, scalar2=None,
                                                             op0=ALU.max), reads=[acc_b], writes=[acc_b])
            S.op("dve", lambda e, dacc=dacc: e.reciprocal(out=dacc, in_=dacc), reads=[acc_b], writes=[acc_b])
            S.op("dve", lambda e, dacc=dacc, oacc=oacc: e.tensor_tensor(out=oab, in0=oacc, in1=dacc, op=ALU.mult),
                 reads=[acc_b], writes=[oab_b])
            S.dma(OAv[:, :, u0:u0 + 2048], oab, oab_b, False)
        S.barrier()

    def lru_s1(l, n, dirn, xc, xc_b, W, wk, LWt, LW_b, pcnt):
        c = cst[l]
        xcb, r_, i_, a_ = wk["xcb"], wk["r"], wk["i"], wk["a"]
        wb = wk["buf"]
        S.op("pool", lambda e: e.tensor_copy(out=xcb[:, 0:W], in_=xc[:, 0:W]),
             reads=[xc_b], writes=[wb["xcb"]])
        for gate, dst, dkey in ((0, r_, "r"), (1, i_, "i")):
            for c0 in range(0, W, 512):
                cw_ = min(512, W - c0)
                bk = pcnt[0] % NBK
                pcnt[0] += 1
                S.op("pe", lambda e, bk=bk, c0=c0, cw_=cw_, gate=gate: e.matmul(
                    banks[bk][:, 0:cw_], lhsT=LWt[:, dirn * 2 + gate, n, :], rhs=xcb[:, c0:c0 + cw_],
                    start=True, stop=True), reads=[wb["xcb"], LW_b], writes=[pbuf[bk]])
                col = (dirn * 2 + gate) * 16 + n
                S.op("act", lambda e, bk=bk, c0=c0, cw_=cw_, dst=dst, col=col: e.activation(
                    out=dst[:, c0:c0 + cw_], in_=banks[bk][:, 0:cw_], func=AF.Sigmoid,
                    bias=c["lb"][:, col:col + 1]), reads=[pbuf[bk], c["buf"]], writes=[wb[dkey]])
        cc = dirn * 16 + n
        S.op("act", lambda e: e.activation(out=a_[:, 0:W], in_=r_[:, 0:W], func=AF.Exp,
                                           scale=c["cf"][:, cc:cc + 1]), reads=[wb["r"], c["buf"]], writes=[wb["a"]])

    def lru_s2(l, n, dirn, xc, xc_b, W, wk, vt, vt_b):
        i_, a_, a2 = wk["i"], wk["a"], wk["a2"]
        wb = wk["buf"]
        S.op("dve", lambda e: e.tensor_tensor(out=a2[:, 0:W], in0=a_[:, 0:W], in1=a_[:, 0:W], op=ALU.mult),
             reads=[wb["a"]], writes=[wb["a2"]])
        S.op("act", lambda e: e.activation(out=a2[:, 0:W], in_=a2[:, 0:W], func=AF.Sqrt, scale=-1.0, bias=1.0),
             reads=[wb["a2"]], writes=[wb["a2"]])
        S.op("dve", lambda e: e.tensor_tensor(out=i_[:, 0:W], in0=i_[:, 0:W], in1=xc[:, 0:W], op=ALU.mult),
             reads=[wb["i"], xc_b], writes=[wb["i"]])
        if vt is not None:
            S.op("pool", lambda e: e.tensor_tensor(out=i_[:, 0:W], in0=i_[:, 0:W], in1=vt[:, 0:W], op=ALU.mult),
                 reads=[wb["i"], vt_b], writes=[wb["i"]])

    def lru_s3(l, n, dirn, W, wk, carry, carry_b, h, h_b):
        i_, a_, a2 = wk["i"], wk["a"], wk["a2"]
        wb = wk["buf"]
        S.op("pool", lambda e: e.tensor_tensor(out=a2[:, 0:W], in0=a2[:, 0:W], in1=i_[:, 0:W], op=ALU.mult),
             reads=[wb["a2"], wb["i"]], writes=[wb["a2"]])
        if dirn == 0:
            S.op("dve", lambda e: e.tensor_tensor_scan(out=h[:, 0:W], data0=a_[:, 0:W], data1=a2[:, 0:W],
                                                       initial=carry[:, n:n + 1], op0=ALU.mult, op1=ALU.add),
                 reads=[wb["a"], wb["a2"], carry_b], writes=[h_b])
            S.op("dve", lambda e: e.tensor_copy(out=carry[:, n:n + 1], in_=h[:, W - 1:W]),
                 reads=[h_b], writes=[carry_b])
        else:
            S.op("dve", lambda e: e.tensor_tensor_scan(out=h[:, 0:W][:, ::-1], data0=a_[:, 0:W][:, ::-1],
                                                       data1=a2[:, 0:W][:, ::-1], initial=carry[:, n:n + 1],
                                                       op0=ALU.mult, op1=ALU.add),
                 reads=[wb["a"], wb["a2"], carry_b], writes=[h_b])
            S.op("dve", lambda e: e.tensor_copy(out=carry[:, n:n + 1], in_=h[:, 0:1]),
                 reads=[h_b], writes=[carry_b])

    def lru_work():
        wk = dict(xcb=A.bf16(1032), r=A.f32(1032), i=A.f32(1032), a=A.f32(1032), a2=A.f32(1032))
        wk["buf"] = {k: Buf("wk_" + k) for k in ("xcb", "r", "i", "a", "a2")}
        return wk

    def phase_fwd(l):
        A.reset()
        c = cst[l]
        TT = 1024
        NT = T // TT
        xt = [A.bf16(16 * TT).rearrange("p (c t) -> p c t", c=16)] * 2
        xt_b = [Buf("fxt")] * 2
        NWF = 4
        wf = [A.bf16(16 * 128).rearrange("p (c n) -> p c n", c=16) for _ in range(NWF)]
        wf_b = [Buf("wf%d" % i) for i in range(NWF)]
        LWt = A.bf16(4 * 16 * 128).rearrange("p (a n d) -> p a n d", a=4, n=16)
        LW_b = Buf("LWt")
        S.dma(LWt, LW[l], LW_b, True)
        stg = [A.bf16(TT) for _ in range(3)]
        stg_b = [Buf("stg%d" % i) for i in range(3)]
        NXB = 3
        XRb = [A.f32(1032) for _ in range(NXB)]
        XRb_b = [Buf("XRb%d" % i) for i in range(NXB)]
        xc = [A.f32(1032) for _ in range(NXB)]
        xc_b = [Buf("xc%d" % i) for i in range(NXB)]
        hh = [A.f32(1032) for _ in range(NXB)]
        hh_b = [Buf("hh%d" % i) for i in range(NXB)]
        vt = [A.f32(TT)] * 2
        vt_b = [Buf("vt")] * 2
        xrc = A.f32(48)
        xrc_b = Buf("xrc")
        hc = A.f32(16)
        hc_b = Buf("hc")
        S.op("pool", lambda e: e.memset(xrc, 0.0), writes=[xrc_b])
        S.op("pool", lambda e: e.memset(hc, 0.0), writes=[hc_b])
        for i in range(NXB):
            S.op("pool", lambda e, i=i: e.memset(XRb[i], 0.0), writes=[XRb_b[i]])
        wks = [lru_work() for _ in range(3)]
        XTv = XT.rearrange("(c p) t -> p c t", p=128)
        pcnt = [0]
        cgc = 0
        sc = 0
        xc_i = 0
        seq = []
        for n in range(16):
            seq += [("xr", n, 4 + n), ("silu", n, 20 + n), ("sig", 2 * n, 36 + 2 * n), ("sig", 2 * n + 1, 37 + 2 * n)]
        seq += [("ga", i, i) for i in range(4)]
        NS = len(seq)
        allw = [(it, p) for it in range(NT) for p in range(NS)]

        def wload(j):
            it, p = allw[j]
            S.dma(wf[j % NWF], WF[l][seq[p][2]], wf_b[j % NWF], True)

        wload(0)
        wload(1)
        for it in range(NT):
            t0 = it * TT
            last = it == NT - 1
            k = it % 2
            S.dma(xt[k], XTv[:, :, t0:t0 + TT], xt_b[k], True)
            S.dma(vt[k], valid[0:1, t0:t0 + TT].partition_broadcast(128), vt_b[k], True)
            sched = {}
            for p, (kind, idx, cidx) in enumerate(seq):
                j = it * NS + p
                wsl = j % NWF
                if j + 2 < len(allw):
                    wload(j + 2)
                if kind == "xr":
                    xi = xc_i % NXB
                    xc_i += 1
                else:
                    si = sc % 3
                    sc += 1
                for s in range(TT // 512):
                    bk = pcnt[0] % NBK
                    pcnt[0] += 1
                    for kc in range(16):
                        S.op("pe", lambda e, bk=bk, wsl=wsl, k=k, kc=kc, s=s: e.matmul(
                            banks[bk][:, :], lhsT=wf[wsl][:, kc, :],
                            rhs=xt[k][:, kc, s * 512:(s + 1) * 512], start=(kc == 0), stop=(kc == 15)),
                            reads=[wf_b[wsl], xt_b[k]], writes=[pbuf[bk]])
                    bias = c["bcol"][:, cidx:cidx + 1]
                    if kind == "xr":
                        S.op("act", lambda e, bk=bk, xi=xi, s=s, bias=bias: e.activation(
                            out=XRb[xi][:, 3 + s * 512:3 + (s + 1) * 512], in_=banks[bk][:, :],
                            func=AF.Identity, bias=bias), reads=[pbuf[bk], c["buf"]], writes=[XRb_b[xi]])
                    else:
                        fn_ = AF.Sigmoid if kind == "sig" else AF.Silu
                        S.op("act", lambda e, bk=bk, si=si, s=s, bias=bias, fn_=fn_: e.activation(
                            out=stg[si][:, s * 512:(s + 1) * 512], in_=banks[bk][:, :], func=fn_, bias=bias),
                            reads=[pbuf[bk], c["buf"]], writes=[stg_b[si]])
                for fnc in sched.pop(p, []):
                    fnc()
                if kind != "xr":
                    dst = {"ga": SGA, "silu": SGR, "sig": GM}[kind]
                    S.dma(dst[idx * 128:(idx + 1) * 128, t0:t0 + TT], stg[si], stg_b[si], False, eng="act")
                    continue
                n = idx
                X = XRb[xi]
                Xb = XRb_b[xi]
                S.op("pool", lambda e, X=X, n=n: e.tensor_copy(out=X[:, 0:3], in_=xrc[:, n * 3:(n + 1) * 3]),
                     reads=[xrc_b], writes=[Xb])
                S.op("pool", lambda e, X=X, k=k: e.tensor_tensor(out=X[:, 3:3 + TT], in0=X[:, 3:3 + TT],
                                                                in1=vt[k], op=ALU.mult),
                     reads=[Xb, vt_b[k]], writes=[Xb])
                S.op("pool", lambda e, X=X, n=n: e.tensor_copy(out=xrc[:, n * 3:(n + 1) * 3], in_=X[:, TT:TT + 3]),
                     reads=[Xb], writes=[xrc_b])
                wlo = 1 if it == 0 else 0
                whi = TT + (1 if last else 0)
                W = whi - wlo
                tok0 = t0 - 1 + wlo
                xcc = xc[xi]
                cwv = c["cw"]
                S.op("pool", lambda e, X=X, xcc=xcc, n=n, wlo=wlo, W=W: e.tensor_scalar(
                    out=xcc[:, 0:W], in0=X[:, wlo:wlo + W], scalar1=cwv[:, n:n + 1],
                    scalar2=c["cb"][:, n:n + 1], op0=ALU.mult, op1=ALU.add),
                    reads=[Xb, c["buf"]], writes=[xc_b[xi]])
                for jt in range(1, 4):
                    S.op("dve", lambda e, X=X, xcc=xcc, n=n, wlo=wlo, W=W, jt=jt: e.scalar_tensor_tensor(
                        out=xcc[:, 0:W], in0=X[:, wlo + jt:wlo + jt + W], scalar=cwv[:, jt * 16 + n:jt * 16 + n + 1],
                        in1=xcc[:, 0:W], op0=ALU.mult, op1=ALU.add),
                        reads=[Xb, c["buf"], xc_b[xi]], writes=[xc_b[xi]])
                S.dma(XC[n * 128:(n + 1) * 128, tok0:tok0 + W], xcc[:, 0:W], xc_b[xi], False, eng="pool")
                wk = wks[n % 3]

                def f1(n=n, xi=xi, W=W, wk=wk):
                    lru_s1(l, n, 0, xc[xi], xc_b[xi], W, wk, LWt, LW_b, pcnt)

                def f2(n=n, xi=xi, W=W, wk=wk):
                    lru_s2(l, n, 0, xc[xi], xc_b[xi], W, wk, None, None)

                def f3(n=n, xi=xi, W=W, wk=wk, tok0=tok0):
                    lru_s3(l, n, 0, W, wk, hc, hc_b, hh[xi], hh_b[xi])
                    S.dma(HF[n * 128:(n + 1) * 128, tok0:tok0 + W], hh[xi][:, 0:W], hh_b[xi], False, eng="pool")

                for dl, f in ((2, f1), (3, f2), (4, f3)):
                    sched.setdefault(min(p + dl, NS - 1), []).append(f)
            for p in sorted(sched):
                for fnc in sched[p]:
                    fnc()
        S.barrier()

    def phase_bwd(l):
        A.reset()
        TT = 1024
        NT = T // TT
        LWt = A.bf16(4 * 16 * 128).rearrange("p (a n d) -> p a n d", a=4, n=16)
        LW_b = Buf("LWt")
        S.dma(LWt, LW[l], LW_b, True)
        NBB = 6
        xc = [A.f32(TT) for _ in range(NBB)]
        xc_b = [Buf("bxc%d" % i) for i in range(NBB)]
        hf = [A.f32(TT) for _ in range(NBB)]
        hf_b = [Buf("bhf%d" % i) for i in range(NBB)]
        sg = [A.bf16(TT) for _ in range(NBB)]
        sg_b = [Buf("bsg%d" % i) for i in range(NBB)]
        hh = [A.f32(TT) for _ in range(2)]
        hh_b = [Buf("bhh%d" % i) for i in range(2)]
        og = [A.bf16(TT) for _ in range(2)]
        og_b = [Buf("bog%d" % i) for i in range(2)]
        vt = [A.f32(TT) for _ in range(2)]
        vt_b = [Buf("bvt%d" % i) for i in range(2)]
        hc = A.f32(16)
        hc_b = Buf("bhc")
        S.op("pool", lambda e: e.memset(hc, 0.0), writes=[hc_b])
        wks = [lru_work() for _ in range(3)]
        pcnt = [0]
        work = [(ii, it, n) for ii, it in enumerate(reversed(range(NT))) for n in range(16)]
        NWK = len(work)

        def loads(j):
            ii, it, n = work[j]
            k = j % NBB
            t0 = it * TT
            rows = slice(n * 128, (n + 1) * 128)
            if n == 0:
                S.dma(vt[ii % 2], valid[0:1, t0:t0 + TT].partition_broadcast(128), vt_b[ii % 2], True)
            S.dma(xc[k], XC[rows, t0:t0 + TT], xc_b[k], True)
            S.dma(hf[k], HF[rows, t0:t0 + TT], hf_b[k], True)
            S.dma(sg[k], SGR[rows, t0:t0 + TT], sg_b[k], True)

        loads(0)
        loads(1)
        for j in range(NWK + 2):
            if j + 2 < NWK:
                loads(j + 2)
            if j < NWK:
                ii, it, n = work[j]
                lru_s1(l, n, 1, xc[j % NBB], xc_b[j % NBB], TT, wks[j % 3], LWt, LW_b, pcnt)
            if 0 <= j - 1 < NWK:
                ii, it, n = work[j - 1]
                k = (j - 1) % NBB
                lru_s2(l, n, 1, xc[k], xc_b[k], TT, wks[(j - 1) % 3], vt[ii % 2], vt_b[ii % 2])
            if 0 <= j - 2 < NWK:
                jj = j - 2
                ii, it, n = work[jj]
                k = jj % NBB
                q = jj % 2
                t0 = it * TT
                rows = slice(n * 128, (n + 1) * 128)
                lru_s3(l, n, 1, TT, wks[jj % 3], hc, hc_b, hh[q], hh_b[q])
                S.op("pool", lambda e, k=k, q=q: e.tensor_tensor(out=hf[k], in0=hf[k], in1=hh[q], op=ALU.add),
                     reads=[hf_b[k], hh_b[q]], writes=[hf_b[k]])
                S.op("dve", lambda e, k=k, q=q: e.tensor_tensor(out=og[q], in0=hf[k], in1=sg[k], op=ALU.mult),
                     reads=[hf_b[k], sg_b[k]], writes=[og_b[q]])
                S.dma(HG[rows, t0:t0 + TT], og[q], og_b[q], False)
        S.barrier()

    def phase_merge(l):
        A.reset()
        TT = 512
        wao = A.bf16(4 * D).rearrange("p (c n) -> p c n", c=4)
        wro = A.bf16(16 * D).rearrange("p (c n) -> p c n", c=16)
        w_b = Buf("w4a")
        S.dma(wao, WAO[l], w_b, True)
        S.dma(wro, WRO[l], w_b, True)
        hg = [A.bf16(16 * TT).rearrange("p (c t) -> p c t", c=16) for _ in range(2)]
        hg_b = [Buf("hg%d" % i) for i in range(2)]
        oa = [A.bf16(4 * TT).rearrange("p (c t) -> p c t", c=4) for _ in range(2)]
        oa_b = [Buf("oa%d" % i) for i in range(2)]
        sga = [A.bf16(4 * TT).rearrange("p (c t) -> p c t", c=4) for _ in range(2)]
        sga_b = [Buf("sga%d" % i) for i in range(2)]
        gm = [A.bf16(2 * TT).rearrange("p (c t) -> p c t", c=2) for _ in range(4)]
        gm_b = [Buf("gm%d" % i) for i in range(4)]
        GMv = GM.rearrange("(a c p) t -> p a c t", a=2, p=128)
        gc = 0
        mt = [A.bf16(16 * TT).rearrange("p (c t) -> p c t", c=16) for _ in range(2)]
        mt_b = [Buf("mt%d" % i) for i in range(2)]
        t1 = [A.f32(TT) for _ in range(2)]
        t1_b = [Buf("t1%d" % i) for i in range(2)]
        t2 = [A.f32(TT) for _ in range(2)]
        t2_b = [Buf("t2%d" % i) for i in range(2)]
        pc = 0
        tc = 0
        NTm = T // TT

        def tloads(it):
            t0 = it * TT
            k = it % 2
            S.dma(hg[k], HG.rearrange("(c p) t -> p c t", p=128)[:, :, t0:t0 + TT], hg_b[k], True)
            S.dma(oa[k], OA.rearrange("(c p) t -> p c t", p=128)[:, :, t0:t0 + TT], oa_b[k], True)
            S.dma(sga[k], SGA.rearrange("(c p) t -> p c t", p=128)[:, :, t0:t0 + TT], sga_b[k], True)

        gwork = [(it, ft) for it in range(NTm) for ft in range(16)]

        def gload(j):
            it, ft = gwork[j]
            S.dma(gm[j % 4], GMv[:, :, ft, it * TT:(it + 1) * TT], gm_b[j % 4], True)

        tloads(0)
        gload(0)
        gload(1)
        for it in range(NTm):
            t0 = it * TT
            k = it % 2
            if it + 1 < NTm:
                tloads(it + 1)
            S.op("pool", lambda e, k=k: e.tensor_tensor(out=oa[k], in0=oa[k], in1=sga[k], op=ALU.mult),
                 reads=[oa_b[k], sga_b[k]], writes=[oa_b[k]])
            for ft in range(16):
                gj = it * 16 + ft
                gk = gj % 4
                if gj + 2 < len(gwork):
                    gload(gj + 2)
                ba = pc % NBK
                bb = (pc + 1) % NBK
                pc += 2
                for kc in range(4):
                    S.op("pe", lambda e, ba=ba, kc=kc, ft=ft, k=k: e.matmul(
                        banks[ba][:, :], lhsT=wao[:, kc, ft * 128:(ft + 1) * 128], rhs=oa[k][:, kc, :],
                        start=(kc == 0), stop=(kc == 3)), reads=[w_b, oa_b[k]], writes=[pbuf[ba]])
                for kc in range(16):
                    S.op("pe", lambda e, bb=bb, kc=kc, ft=ft, k=k: e.matmul(
                        banks[bb][:, :], lhsT=wro[:, kc, ft * 128:(ft + 1) * 128], rhs=hg[k][:, kc, :],
                        start=(kc == 0), stop=(kc == 15)), reads=[w_b, hg_b[k]], writes=[pbuf[bb]])
                q = tc % 2
                tc += 1
                S.op("dve", lambda e, ba=ba, q=q, gk=gk: e.tensor_tensor(
                    out=t1[q], in0=banks[ba][:, :], in1=gm[gk][:, 0, :], op=ALU.mult),
                    reads=[pbuf[ba], gm_b[gk]], writes=[t1_b[q]])
                S.op("dve", lambda e, bb=bb, q=q, gk=gk: e.tensor_tensor(
                    out=t2[q], in0=banks[bb][:, :], in1=gm[gk][:, 1, :], op=ALU.mult),
                    reads=[pbuf[bb], gm_b[gk]], writes=[t2_b[q]])
                S.op("pool", lambda e, q=q, k=k, ft=ft: e.tensor_tensor(
                    out=mt[k][:, ft, :], in0=t1[q], in1=t2[q], op=ALU.add),
                    reads=[t1_b[q], t2_b[q]], writes=[mt_b[k]])
            S.dma(MT.rearrange("(c p) t -> p c t", p=128)[:, :, t0:t0 + TT], mt[k], mt_b[k], False, eng="act")
        S.barrier()

    def phase_out(l, Xtok, Ydst, make_xt):
        A.reset()
        wo = A.bf16(16 * D).rearrange("p (c n) -> p c n", c=16)
        wo_b = Buf("wo")
        S.dma(wo, WO[l], wo_b, True)
        gt = A.f32(D)
        bt = A.f32(D)
        gb_b = Buf("gb")
        S.dma(gt, ln_g[l:l + 1, :].partition_broadcast(128), gb_b, True)
        S.dma(bt, ln_b[l:l + 1, :].partition_broadcast(128), gb_b, True)
        mt = [A.bf16(16 * 512).rearrange("p (c t) -> p c t", c=16) for _ in range(2)]
        mt_b = [Buf("omt%d" % i) for i in range(2)]
        xs = [A.f32(D) for _ in range(2)]
        xs_b = [Buf("oxs%d" % i) for i in range(2)]
        yt = [A.f32(2 * D).rearrange("p (s f) -> p s f", s=2) for _ in range(2)]
        yt_b = [Buf("oyt%d" % i) for i in range(2)]
        xts = [A.bf16(16 * 512).rearrange("p (c t) -> p c t", c=16) for _ in range(1)]
        xts_b = [Buf("oxts%d" % i) for i in range(1)]
        st = [A.f32(4 * 6) for _ in range(2)]
        mv = [A.f32(8) for _ in range(2)]
        st_b = [Buf("ost%d" % i) for i in range(2)]
        xc_ = 0
        ev = [0]
        NTo = T // 512
        S.dma(mt[0], MT.rearrange("(c p) t -> p c t", p=128)[:, :, 0:512], mt_b[0], True)
        S.dma(xs[0], Xtok[0:128, :], xs_b[0], True)
        for it in range(NTo):
            t0 = it * 512
            k = it % 2
            if it + 1 < NTo:
                S.dma(mt[1 - k], MT.rearrange("(c p) t -> p c t", p=128)[:, :, t0 + 512:t0 + 1024], mt_b[1 - k], True)
            for s in range(4):
                xk = xc_ % 2
                xc_ += 1
                tk = t0 + s * 128
                if tk + 128 < T:
                    S.dma(xs[1 - xk], Xtok[tk + 128:tk + 256, :], xs_b[1 - xk], True)
                yk = (it * 2 + s // 2) % 2
                yv = yt[yk][:, s % 2, :]
                for nch in range(4):
                    for kc in range(16):
                        S.op("pe", lambda e, nch=nch, kc=kc, k=k, s=s: e.matmul(
                            banks[nch][:, :], lhsT=mt[k][:, kc, s * 128:(s + 1) * 128],
                            rhs=wo[:, kc, nch * 512:(nch + 1) * 512], start=(kc == 0), stop=(kc == 15)),
                            reads=[mt_b[k], wo_b], writes=[pbuf[nch]])
                    S.op("dve", lambda e, nch=nch, xk=xk, yv=yv: e.scalar_tensor_tensor(
                        out=yv[:, nch * 512:(nch + 1) * 512], in0=xs[xk][:, nch * 512:(nch + 1) * 512], scalar=ALPHA,
                        in1=banks[nch][:, :], op0=ALU.mult, op1=ALU.add),
                        reads=[xs_b[xk], pbuf[nch]], writes=[yt_b[yk]])
                q = xk
                for nch in range(4):
                    S.op("dve", lambda e, nch=nch, q=q, yv=yv: e.bn_stats(
                        out=st[q][:, nch * 6:(nch + 1) * 6], in_=yv[:, nch * 512:(nch + 1) * 512]),
                        reads=[yt_b[yk]], writes=[st_b[q]])
                S.op("dve", lambda e, q=q: e.bn_aggr(out=mv[q][:, 0:2], in_=st[q].rearrange("p (c s) -> p c s", c=4)),
                     reads=[st_b[q]], writes=[st_b[q]])
                S.op("act", lambda e, q=q: e.activation(out=mv[q][:, 2:3], in_=mv[q][:, 1:2], func=AF.Sqrt,
                                                        bias=LN_EPS), reads=[st_b[q]], writes=[st_b[q]])
                S.op("dve", lambda e, q=q: e.reciprocal(out=mv[q][:, 2:3], in_=mv[q][:, 2:3]),
                     reads=[st_b[q]], writes=[st_b[q]])
                S.op("dve", lambda e, q=q: e.scalar_tensor_tensor(out=mv[q][:, 3:4], in0=mv[q][:, 0:1], scalar=-1.0,
                                                                  in1=mv[q][:, 2:3], op0=ALU.mult, op1=ALU.mult),
                     reads=[st_b[q]], writes=[st_b[q]])
                S.op("act", lambda e, q=q, yv=yv: e.activation(out=yv, in_=yv, func=AF.Identity,
                                                               scale=mv[q][:, 2:3], bias=mv[q][:, 3:4]),
                     reads=[yt_b[yk], st_b[q]], writes=[yt_b[yk]])
                S.op("pool", lambda e, yv=yv: e.tensor_tensor(out=yv, in0=yv, in1=gt, op=ALU.mult),
                     reads=[yt_b[yk], gb_b], writes=[yt_b[yk]])
                S.op("pool", lambda e, yv=yv: e.tensor_tensor(out=yv, in0=yv, in1=bt, op=ALU.add),
                     reads=[yt_b[yk], gb_b], writes=[yt_b[yk]])
                if s % 2 == 1:
                    th = t0 + (s // 2) * 256
                    S.dma(Ydst[th:th + 256, :].rearrange("(s p) f -> p s f", p=128), yt[yk], yt_b[yk], False, eng="act")
                    if make_xt:
                        emit_xt_from_tile(yt[yk], yt_b[yk], th, 2, xts[0], xts_b[0], ev, xbanks=(4, 5))
        S.barrier()

    STOP = stop_after
    def run_all():
        precast()
        phase_xt0()
        for l in range(DEPTH):
            Xtok = x_in if l == 0 else Y1
            Ydst = Y1 if l < DEPTH - 1 else y_out
            for nm, fn in (("qkv", lambda: phase_qkv(l)), ("attn", lambda: phase_attn(l)),
                           ("fwd", lambda: phase_fwd(l)), ("bwd", lambda: phase_bwd(l)),
                           ("merge", lambda: phase_merge(l)),
                           ("out", lambda: phase_out(l, Xtok, Ydst, l < DEPTH - 1))):
                fn()
                if STOP == (l, nm):
                    return
    run_all()

    S.finalize(es)
    S.emit()
    return nc, es


def t5_bucket(rel):
    nb = 16
    max_exact = 8
    ret = (rel > 0).astype(np.int32) * nb
    n = np.abs(rel)
    large = max_exact + (np.log(np.maximum(n, max_exact) / max_exact)
                         / np.log(1024 / max_exact) * (nb - max_exact)).astype(np.int32)
    large = np.minimum(large, nb - 1)
    return (ret + np.where(n < max_exact, n, large)).astype(np.int32)


def host_constants(rel_bias, T, seq_len):
    eb = np.full((3, 2, 128, 4, 128), NEG, np.float32)
    kl = np.arange(128)[:, None]
    ql = np.arange(128)[None, :]
    for g, d in enumerate(DILS):
        tab = rel_bias[t5_bucket(np.arange(-64, 65) * d)][:, g * 4:(g + 1) * 4]
        offA = kl - ql - 64
        offB = kl - ql + 64
        for ab, off in enumerate((offA, offB)):
            ok = np.abs(off) <= 64
            idx = np.clip(off, -64, 64) + 64
            for h in range(4):
                eb[g, ab, :, h, :] = np.where(ok, tab[idx, h], NEG)
    ncol = 3 * (T // 128) + 21
    km = np.zeros((128, ncol), np.float32)
    col = 0
    for g, d in enumerate(DILS):
        L = T // d
        nb = L // 128
        for r in range(d):
            for j in range(nb + 1):
                p = 128 * j - 64 + np.arange(128)
                tok = p * d + r
                ok = (p >= 0) & (p < L) & (tok < seq_len)
                km[:, col] = np.where(ok, 0.0, NEG)
                col += 1
    assert col == ncol
    valid = (np.arange(T) < seq_len).astype(np.float32)[None, :]
    return {"ebias": eb, "kmask": km, "valid": valid, "ident": np.eye(128, dtype=np.float32)}


WKEYS = ("w_in", "b_in", "conv_w", "conv_b", "lru_w", "lru_b", "lru_lam", "w_attn_o", "w_rnn_o",
         "w_out", "ln_g", "ln_b")


def core_inputs(x_seq, T, weights, rel_bias):
    seq_len = x_seq.shape[0]
    xp = np.zeros((T, D), np.float32)
    xp[:seq_len] = x_seq
    m = {"x": xp}
    for k in WKEYS:
        m[k] = np.ascontiguousarray(weights[k], dtype=np.float32)
    m.update(host_constants(np.asarray(rel_bias, np.float32), T, seq_len))
    return m


_CACHE = {}


def kernel(**inputs):
    T = 16384
    if T not in _CACHE:
        _CACHE[T] = build_program(T)
    nc, _es = _CACHE[T]
    weights = {k: np.asarray(inputs[k]) for k in WKEYS}
    xp = np.asarray(inputs["x_prompt"])
    xs = np.asarray(inputs["x_sample"])
    seqs = [xp[0], xs[0], xs[1]]
    in_maps = [core_inputs(s, T, weights, inputs["rel_bias"]) for s in seqs]
    res = run_bass_kernel_spmd(nc, in_maps, core_ids=[0, 1, 2])
    y_p = np.asarray(res.results[0]["y"], dtype=np.float32)[:xp.shape[1]][None]
    y_s = np.stack([np.asarray(res.results[1]["y"], dtype=np.float32),
                    np.asarray(res.results[2]["y"], dtype=np.float32)])
    return (np.ascontiguousarray(y_p), np.ascontiguousarray(y_s))
```
